# Optimizing a Trainium2 kernel written in Bass

```python
import jax, jax.numpy as jnp
from jax import lax
import numpy as np

D_MODEL = 1024
BATCH = 4
SEQ = 4096
DEPTH = 1

MIX_WIDTH = D_MODEL
CONV_WIDTH = MIX_WIDTH // 2
CONV_GROUPS = 8
CONV_K = 3
MLSTM_HEADS = 4
MLSTM_V_DIM = (MIX_WIDTH - CONV_WIDTH) // MLSTM_HEADS
MLSTM_QK_DIM = MLSTM_V_DIM // 2
CHUNK = 128
D_FF = 2816
FFN_RESIDUAL_SCALE = 0.5
EPS = 1e-6
NEG_INF = -1e30
IN_SIZES = (CONV_WIDTH, CONV_WIDTH, CONV_WIDTH,
            MLSTM_HEADS * MLSTM_QK_DIM, MLSTM_HEADS * MLSTM_QK_DIM,
            MLSTM_HEADS * MLSTM_V_DIM, MLSTM_HEADS * MLSTM_V_DIM,
            2 * MLSTM_HEADS, 2 * MLSTM_HEADS)
W_IN_COLS = sum(IN_SIZES)

kernel_name = "hybrid_conv_mlstm_macaron_sandwich_encoder"


def rmsnorm(x, g):
    x32 = x.astype(jnp.float32)
    y = x32 * lax.rsqrt(jnp.mean(x32 * x32, axis=-1, keepdims=True) + EPS)
    return (y * g.astype(jnp.float32)).astype(x.dtype)


def swiglu(x, w_in, w_out):
    gate, up = jnp.split(x @ w_in, 2, axis=-1)
    return (jax.nn.silu(gate) * up) @ w_out


def mlstm_chunkwise(q, k, v, li, lf):
    B, H, S, DK = q.shape
    DV = v.shape[-1]
    NC = S // CHUNK
    q = q.reshape(B, H, NC, CHUNK, DK)
    k = k.reshape(B, H, NC, CHUNK, DK)
    v = v.reshape(B, H, NC, CHUNK, DV)
    li = li.reshape(B, H, NC, CHUNK)
    lf = lf.reshape(B, H, NC, CHUNK)
    b = jnp.cumsum(lf, axis=-1)
    g = b[..., -1]
    causal = jnp.tril(jnp.ones((CHUNK, CHUNK), dtype=bool))
    D = jnp.where(causal, b[..., :, None] - b[..., None, :] + li[..., None, :], NEG_INF)
    m_intra = jnp.max(D, axis=-1)
    w = g[..., None] - b + li
    m_chunk = jnp.max(w, axis=-1)
    e = jnp.exp(w - m_chunk[..., None])
    C_chunk = jnp.einsum('bhcl,bhclk,bhclv->bhckv', e, k, v)
    n_chunk = jnp.einsum('bhcl,bhclk->bhck', e, k)

    def step(carry, inp):
        C, n, m = carry
        g_c, Cc, nc, mc = inp
        m_new = jnp.maximum(g_c + m, mc)
        a = jnp.exp(g_c + m - m_new)
        s = jnp.exp(mc - m_new)
        C_new = a[..., None, None] * C + s[..., None, None] * Cc
        n_new = a[..., None] * n + s[..., None] * nc
        return (C_new, n_new, m_new), (C, n, m)

    init = (jnp.zeros((B, H, DK, DV), jnp.float32),
            jnp.zeros((B, H, DK), jnp.float32),
            jnp.full((B, H), NEG_INF, jnp.float32))
    xs = (jnp.moveaxis(g, 2, 0), jnp.moveaxis(C_chunk, 2, 0),
          jnp.moveaxis(n_chunk, 2, 0), jnp.moveaxis(m_chunk, 2, 0))
    _, (C_prev, n_prev, m_prev) = lax.scan(step, init, xs)
    C_prev = jnp.moveaxis(C_prev, 0, 2)
    n_prev = jnp.moveaxis(n_prev, 0, 2)
    m_prev = jnp.moveaxis(m_prev, 0, 2)

    m_inter = b + m_prev[..., None]
    m_t = jnp.maximum(m_inter, m_intra)
    P = jnp.exp(D - m_t[..., None]) * jnp.einsum('bhctk,bhcsk->bhcts', q, k)
    inter = jnp.exp(m_inter - m_t)
    numer = inter[..., None] * jnp.einsum('bhctk,bhckv->bhctv', q, C_prev) \
        + jnp.einsum('bhcts,bhcsv->bhctv', P, v)
    denom = inter * jnp.einsum('bhctk,bhck->bhct', q, n_prev) + jnp.sum(P, axis=-1)
    h = numer / jnp.maximum(jnp.abs(denom), jnp.exp(-m_t))[..., None]
    return h.reshape(B, H, S, DV)


def mixer(xn, w_in, conv_w, conv_b, gate_i_bias, gate_f_bias, head_gain, w_out):
    Bsz, S, _ = xn.shape
    H, DK, DV = MLSTM_HEADS, MLSTM_QK_DIM, MLSTM_V_DIM
    split_points = [int(p) for p in np.cumsum(IN_SIZES)[:-1]]
    bg, cg, hc, q, k, v, o, ig, fg = jnp.split(xn @ w_in, split_points, axis=-1)

    u = cg * hc
    u = lax.conv_general_dilated(u, conv_w[:, None, :].astype(u.dtype), window_strides=(1,),
                                 padding=[(1, 1)], dimension_numbers=('NWC', 'WIO', 'NWC'),
                                 feature_group_count=CONV_WIDTH)
    y_conv = bg * (u + conv_b.astype(u.dtype))

    def heads(t, d):
        return t.reshape(Bsz, S, H, d).transpose(0, 2, 1, 3).astype(jnp.float32)
    qh = heads(q, DK) * (DK ** -0.5)
    kh = heads(k, DK)
    vh = heads(v, DV)
    li = (ig.astype(jnp.float32) + gate_i_bias.astype(jnp.float32)).reshape(Bsz, S, 2, H).transpose(2, 0, 3, 1)
    lf = jax.nn.log_sigmoid(fg.astype(jnp.float32) + gate_f_bias.astype(jnp.float32)).reshape(Bsz, S, 2, H).transpose(2, 0, 3, 1)
    h_fwd = mlstm_chunkwise(qh, kh, vh, li[0], lf[0])
    flip = lambda t: jnp.flip(t, axis=2)
    h_bwd = flip(mlstm_chunkwise(flip(qh), flip(kh), flip(vh), flip(li[1]), flip(lf[1])))
    h = h_fwd + h_bwd
    h = h * lax.rsqrt(jnp.mean(h * h, axis=-1, keepdims=True) + EPS) \
        * head_gain.astype(jnp.float32).reshape(H, 1, DV)
    h = h.transpose(0, 2, 1, 3).reshape(Bsz, S, H * DV).astype(xn.dtype)
    y_mlstm = jax.nn.sigmoid(o) * h

    return jnp.concatenate([y_conv, y_mlstm], axis=-1) @ w_out


def setup_inputs(seed: int = 0) -> dict:
    key = jax.random.key(seed)
    ks = jax.random.split(key, 20)
    H = MLSTM_HEADS

    def nrm(k, shape, scale):
        return jax.random.normal(k, shape, jnp.float32) * scale

    def gain(k, n):
        return 1.0 + 0.05 * jax.random.normal(k, (DEPTH, n), jnp.float32)

    f_bias = jnp.tile(jnp.linspace(3.0, 6.0, H, dtype=jnp.float32), 2)[None, :] + nrm(ks[12], (DEPTH, 2 * H), 0.1)
    return {
        'x': jax.random.normal(ks[0], (BATCH, SEQ, D_MODEL), jnp.float32),
        'norm_ffn1_pre': gain(ks[1], D_MODEL),
        'norm_ffn1_post': gain(ks[2], D_MODEL),
        'w_ffn1_in': nrm(ks[3], (DEPTH, D_MODEL, 2 * D_FF), D_MODEL ** -0.5),
        'w_ffn1_out': nrm(ks[4], (DEPTH, D_FF, D_MODEL), D_FF ** -0.5),
        'norm_mix_pre': gain(ks[5], D_MODEL),
        'norm_mix_post': gain(ks[6], D_MODEL),
        'w_mix_in': nrm(ks[7], (DEPTH, D_MODEL, W_IN_COLS), D_MODEL ** -0.5),
        'conv_w': nrm(ks[8], (DEPTH, CONV_K, CONV_WIDTH), CONV_K ** -0.5),
        'conv_b': nrm(ks[9], (DEPTH, CONV_WIDTH), 0.02),
        'gate_i_bias': nrm(ks[10], (DEPTH, 2 * H), 0.1),
        'gate_f_bias': f_bias,
        'mlstm_norm': gain(ks[11], H * MLSTM_V_DIM),
        'w_mix_out': nrm(ks[13], (DEPTH, MIX_WIDTH, D_MODEL), MIX_WIDTH ** -0.5),
        'norm_ffn2_pre': gain(ks[14], D_MODEL),
        'norm_ffn2_post': gain(ks[15], D_MODEL),
        'w_ffn2_in': nrm(ks[16], (DEPTH, D_MODEL, 2 * D_FF), D_MODEL ** -0.5),
        'w_ffn2_out': nrm(ks[17], (DEPTH, D_FF, D_MODEL), D_FF ** -0.5),
    }


def reference(x, norm_ffn1_pre, norm_ffn1_post, w_ffn1_in, w_ffn1_out,
              norm_mix_pre, norm_mix_post, w_mix_in, conv_w, conv_b,
              gate_i_bias, gate_f_bias, mlstm_norm, w_mix_out,
              norm_ffn2_pre, norm_ffn2_post, w_ffn2_in, w_ffn2_out):
    for l in range(DEPTH):
        h = swiglu(rmsnorm(x, norm_ffn1_pre[l]), w_ffn1_in[l], w_ffn1_out[l])
        x = x + FFN_RESIDUAL_SCALE * rmsnorm(h, norm_ffn1_post[l])
        h = mixer(rmsnorm(x, norm_mix_pre[l]), w_mix_in[l], conv_w[l], conv_b[l],
                  gate_i_bias[l], gate_f_bias[l], mlstm_norm[l], w_mix_out[l])
        x = x + rmsnorm(h, norm_mix_post[l])
        h = swiglu(rmsnorm(x, norm_ffn2_pre[l]), w_ffn2_in[l], w_ffn2_out[l])
        x = x + FFN_RESIDUAL_SCALE * rmsnorm(h, norm_ffn2_post[l])
    return x
```

```python
import os
import numpy as np
import ml_dtypes
from contextlib import ExitStack
import concourse.bass as bass
import concourse.mybir as mybir
from concourse.bass_utils import run_bass_kernel_spmd

F32 = mybir.dt.float32
BF16 = mybir.dt.bfloat16
AF = mybir.ActivationFunctionType
ALU = mybir.AluOpType
AX = mybir.AxisListType

D = 1024
DFF = 2816
NT = 2048
NCORES = 8
EPS = 1e-6
BIG = 60.0
STAGE_FULL = 9


class Op:
    __slots__ = ("eng", "fn", "deps", "slot", "sig", "need", "idx")


class Prog:
    def __init__(self):
        self.ops = []
        self.lw = {}
        self.rd = {}

    def op(self, eng, fn, r=(), w=(), slot=None):
        o = Op()
        o.eng, o.fn, o.slot, o.sig, o.need, o.idx = eng, fn, slot, None, False, len(self.ops)
        deps = {}
        for k in r:
            d = self.lw.get(k)
            if d is not None:
                deps[d.idx] = d
        for k in w:
            d = self.lw.get(k)
            if d is not None:
                deps[d.idx] = d
            for d in self.rd.get(k, ()):
                deps[d.idx] = d
        for k in r:
            self.rd.setdefault(k, []).append(o)
        for k in w:
            self.lw[k] = o
            self.rd[k] = []
        o.deps = []
        for d in deps.values():
            if d.slot is None and d.eng == eng and eng == "pe":
                continue
            d.need = True
            o.deps.append(d)
        self.ops.append(o)
        return o

    def emit(self, nc, es):
        engs = ["pe", "act", "dve", "pool", "sp"]
        sems = {e: es.enter_context(nc.semaphore("s_" + e)) for e in engs}
        slots = {}
        cnt = {e: 0 for e in engs}
        scnt = {}
        for o in self.ops:
            if o.slot is not None:
                if o.slot not in slots:
                    slots[o.slot] = es.enter_context(nc.semaphore("d_" + o.slot))
                    scnt[o.slot] = 0
                scnt[o.slot] += 16
                o.sig = (slots[o.slot], scnt[o.slot], 16)
            elif o.need:
                cnt[o.eng] += 1
                o.sig = (sems[o.eng], cnt[o.eng], 1)
        streams = {e: [o for o in self.ops if o.eng == e] for e in engs}

        def run(ename, e):
            waited = {}
            for o in streams[ename]:
                for d in o.deps:
                    s, v, _ = d.sig
                    if waited.get(s.num, 0) < v:
                        e.wait_ge(s, v)
                        waited[s.num] = v
                ins = o.fn(e)
                if o.sig is not None and ins is not None:
                    ins.then_inc(o.sig[0], o.sig[2])

        block = es.enter_context(nc.Block())
        block.tensor(lambda e: run("pe", e))
        block.scalar(lambda e: run("act", e))
        block.vector(lambda e: run("dve", e))
        block.gpsimd(lambda e: run("pool", e))
        block.sync(lambda e: run("sp", e))


def bc(ap, pattern):
    return bass.AP(ap.tensor, ap.offset, [list(ap.ap[0])] + [list(x) for x in pattern])


def build(stage=STAGE_FULL):
    nc = bass.Bass("TRN2", target_bir_lowering=False)
    es = ExitStack()
    P = Prog()

    def din(name, shape, dt=F32):
        return nc.dram_tensor(name, shape, dt, kind="ExternalInput").ap()

    xT = din("xT", [128, 8, NT])
    TINY = bool(os.environ.get("KTINY"))
    if TINY:
        w1i = din("w1i", [D, 512]); w1o = din("w1o", [DFF, 128])
    else:
        w1i = din("w1i", [D, 2 * DFF]); w1o = din("w1o", [DFF, D])
    wmi = din("wmi", [D, 3088]); wmo = din("wmo", [D, D])
    w2i = din("w2i", [D, 2 * DFF]); w2o = din("w2o", [DFF, D])
    gains_d = din("gains", [128, 6, 8])
    convp_d = din("convp", [128, 4, 4])
    rowp_d = din("rowp", [128, 528])
    masks_d = din("masks", [128, 5, 128])
    identb_d = din("identb", [128, 128], BF16)
    sel_d = din("sel", [128, 16])
    yT = nc.dram_tensor("yT", [128, 8, NT], F32, kind="ExternalOutput").ap()
    FUSED = stage >= 2
    if FUSED:
        xP = din("xP", [128, 8, NT])
        pay_d = nc.dram_tensor("pay", [128, 524], F32).ap()

    def sb(name, shape, dt):
        return es.enter_context(nc.sbuf_tensor(name, shape, dt))

    X = sb("X", [128, 8, NT], F32)
    R1 = sb("R1", [128, 8192], F32)
    Hreg = sb("H", [128, 24832], BF16)
    Wreg = sb("W", [128, 16384], BF16)
    Mreg = sb("M", [128, 3072], F32)
    Greg = sb("G", [128, 1880], F32)
    STG = sb("STG", [128, 2, 512], F32)
    GAINS = sb("gains_s", [128, 6, 8], F32)
    GH = sb("gh_s", [128, 6, 8], F32)
    CONVP = sb("convp_s", [128, 4, 4], F32)
    ROWP = sb("rowp_s", [128, 528], F32)
    MASKS = sb("masks_s", [128, 5, 128], F32)
    IDENTB = sb("identb_s", [128, 128], BF16)
    SEL = sb("sel_s", [128, 16], F32)
    ONESB = sb("onesb", [128, 128], BF16)
    ONESF = sb("onesf", [128, 128], F32)
    CST = sb("cst", [128, 4], F32)
    PS = es.enter_context(nc.psum_tensor("PS", [128, 8, 512], F32))

    R1b = R1[:].bitcast(BF16)
    XN = R1b[:, 0:8192].rearrange("p (k t) -> p k t", k=8)
    HO = R1[:].rearrange("p (m t) -> p m t", m=8)
    Hh = Hreg[:, 0:22528].rearrange("p (f t) -> p f t", f=22)
    QKT = Hreg[:, 0:8192].rearrange("p (m t) -> p m t", m=4)
    VAUG = Hreg[:, 8192:16448].rearrange("p (c h v) -> p c h v", c=16, h=4)
    SGO = Hreg[:, 16448:24640].rearrange("p (c f) -> p c f", c=16)
    Hf = Hreg[:].bitcast(F32)
    BG = Hf[:, 4096:6144].rearrange("p (c t) -> p c t", c=4)
    U = Hf[:, 6144:8200].rearrange("p (c t) -> p c t", c=4)
    YC = Hreg[:, 16400:18448].rearrange("p (c t) -> p c t", c=4)
    CTMP = Hf[:, 9300:9812]
    Wb = Wreg[:]
    Wf = Wreg[:].bitcast(F32)
    SQ = [Mreg[:, 0:512].bitcast(BF16), Mreg[:, 512:1024].bitcast(BF16)]
    STD = Mreg[:, 1024:2048]
    SIL = [Mreg[:, 2048:2560], Mreg[:, 2560:3072]]

    def G3(off, a, b):
        return Greg[:, off:off + a * b].rearrange("p (a b) -> p a b", a=a)
    GR = G3(0, 16, 16)
    LI = G3(256, 16, 8); LF = G3(384, 16, 8); ZZ = G3(512, 16, 8); AZ = G3(640, 16, 8)
    BC = G3(768, 16, 8); TOT = G3(896, 16, 8); EW = G3(1024, 16, 8); EB = G3(1152, 16, 8)
    EGt = G3(1280, 16, 8); GAF = G3(1408, 16, 8); EWA = G3(1536, 16, 8); LIB = G3(1664, 16, 8)
    UB = G3(1792, 4, 8)
    HALO = G3(1824, 4, 2)
    UBT = Greg[:, 1832:1864]
    DEN = Greg[:, 1864:1868]; RD = Greg[:, 1868:1872]; SSH = Greg[:, 1872:1876]


    EPS_AP = CST[:, 0:1]; NBIG_AP = CST[:, 1:2]; ONE_AP = CST[:, 2:3]

    def kH(lo, hi):
        return [("H", g) for g in range(lo // 128, (hi - 1) // 128 + 1)]

    def kW(lo, hi):
        return [("W", g) for g in range(lo // 256, (hi - 1) // 256 + 1)]

    def kR(lo, hi):
        return [("R1", g) for g in range(lo // 512, (hi - 1) // 512 + 1)]

    def kXN(k, n=1024):
        return kR(k * 512, k * 512 + n // 2)

    def kHO(m):
        return kR(m * 1024, (m + 1) * 1024)

    def kHh(f, hf):
        return kH(f * 1024 + hf * 512, f * 1024 + hf * 512 + 512)

    def kQK(m, c0, c1):
        return kH(m * 2048 + c0, m * 2048 + c1)

    def kQKall(c0, c1):
        return [k for m in range(4) for k in kQK(m, c0, c1)]

    def kVA(c):
        return kH(8192 + c * 516, 8192 + (c + 1) * 516)

    def kSGO(c):
        return kH(16448 + c * 512, 16448 + (c + 1) * 512)

    def kBG(mt):
        return kH(8192 + mt * 1024, 8192 + (mt + 1) * 1024)

    def kU(mt):
        return kH(12288 + mt * 1028, 12288 + (mt + 1) * 1028)

    def kYC(ct):
        return kH(16400 + ct * 512, 16400 + (ct + 1) * 512)

    kCTMP = kH(18600, 19624)

    def dma(eng, out, in_, r, w, slot):
        return P.op(eng, lambda e: e.dma_start(out=out, in_=in_), r=r, w=w, slot=slot)

    def xk(hb, k):
        return ("x", hb, k)

    STGS = [(STG[:, 0, :], [("stg", 0)]), (STG[:, 1, :], [("stg", 1)]),
            (Hf[:, 11264:11776], [("stg", 2)] + kH(22528, 23552)), (Hf[:, 11776:12288], [("stg", 3)] + kH(23552, 24576))]
    wq = {"pend": [], "infl": [], "n": 0, "nslots": 2, "ce": 0}

    def set_slots(n):
        assert not wq["pend"] and not wq["infl"]
        wq["nslots"] = n
        wq["n"] = 0

    def pump():
        while wq["pend"] and len(wq["infl"]) < wq["nslots"]:
            dst_p, src_p, keys, shp = wq["pend"].pop(0)
            sl = wq["n"] % wq["nslots"]
            wq["n"] += 1
            stf, sk = STGS[sl]
            st = stf[:, 0:shp[0] * shp[1]].rearrange("p (a b) -> p a b", a=shp[0])
            P.op("sp", lambda e, st=st, src_p=src_p: e.dma_start(out=st, in_=src_p), r=[], w=sk, slot="stg%d" % sl)
            wq["infl"].append((dst_p, st, sk, keys))

    def enqueue(dst, src, keys):
        A, B = dst.shape[1], dst.shape[2]
        Bc = min(B, 512)
        ra = max(1, 512 // Bc)
        for b0 in range(0, B, Bc):
            for a0 in range(0, A, ra):
                r_ = min(ra, A - a0)
                wq["pend"].append((dst[:, a0:a0 + r_, b0:b0 + Bc], src[:, a0:a0 + r_, b0:b0 + Bc], keys, (r_, Bc)))
        pump()

    def cast_some(n):
        for _ in range(n):
            if not wq["infl"]:
                pump()
            if not wq["infl"]:
                return
            dst_p, st, sk, keys = wq["infl"].pop(0)
            wq["ce"] += 1
            if wq["ce"] % 2:
                P.op("act", lambda e, dst_p=dst_p, st=st: e.activation(out=dst_p, in_=st, func=AF.Copy), r=sk, w=keys)
            else:
                P.op("dve", lambda e, dst_p=dst_p, st=st: e.tensor_copy(out=dst_p, in_=st), r=sk, w=keys)
            pump()

    def cast_all():
        while wq["pend"] or wq["infl"]:
            cast_some(1)

    def wload(dst, src, keys):
        enqueue(dst, src, keys)
        cast_all()

    def load_x(src, eng):
        for b in range(2):
            for k in range(8):
                dma(eng, X[:, k, b * 1024:(b + 1) * 1024], src[:, k, b * 1024:(b + 1) * 1024], [], [xk(2 * b, k), xk(2 * b + 1, k)],
                    "xin%d_%d" % (k, b))

    dma("sp", GAINS[:], gains_d, [], ["gains"], "c0")
    dma("sp", CONVP[:], convp_d, [], ["convp"], "c1")
    dma("sp", ROWP[:], rowp_d, [], ["rowp"], "c2")
    dma("sp", MASKS[:], masks_d, [], ["masks"], "c3")
    dma("sp", IDENTB[:], identb_d, [], ["identb"], "c4")
    dma("sp", SEL[:], sel_d, [], ["sel"], "c5")
    load_x((xP if FUSED else xT), "sp")
    P.op("dve", lambda e: e.memset(ONESB[:], 1.0), w=["onesb"])
    P.op("dve", lambda e: e.memset(ONESF[:], 1.0), w=["onesf"])
    P.op("dve", lambda e: e.memset(CST[:, 0:1], EPS), w=["cst"])
    P.op("dve", lambda e: e.memset(CST[:, 1:2], -BIG), w=["cst"])
    P.op("dve", lambda e: e.memset(CST[:, 2:3], 1.0), w=["cst"])
    for gi, sc in enumerate([1.0, 0.5, 1.0, 1.0, 1.0, 0.5]):
        P.op("dve", lambda e, gi=gi, sc=sc: e.tensor_scalar(out=GH[:, gi, :], in0=GAINS[:, gi, :], scalar1=sc,
                                                          scalar2=None, op0=ALU.mult), r=["gains"], w=["gh"])

    def prenorm(c0, n, gi, xv=None, xkeys=None):
        if xv is None:
            xv = XN[:, :, 0:n]
            xkeys = lambda k: kXN(k, n)
        nb = n // 512
        hbs = [c0 // 512 + i for i in range(nb)]
        for k in range(8):
            sq = SQ[k % 2][:, 0:n]
            P.op("act", lambda e, k=k, sq=sq: e.activation(out=sq, in_=X[:, k, c0:c0 + n], func=AF.Square),
                 r=[xk(h, k) for h in hbs], w=[("sq", k % 2)])
            for hf in range(nb):
                P.op("pe", lambda e, k=k, sq=sq, hf=hf: e.matmul(PS[:, 6 + hf, :], lhsT=ONESB[:], rhs=sq[:, hf * 512:(hf + 1) * 512],
                                                              start=(k == 0), stop=(k == 7)),
                     r=[("sq", k % 2), "onesb"], w=[("ps", 6 + hf)])
        rstd(n)
        for k in range(8):
            P.op("dve", lambda e, k=k: e.scalar_tensor_tensor(out=xv[:, k, :], in0=X[:, k, c0:c0 + n], scalar=GAINS[:, gi, k:k + 1],
                                                            in1=STD[:, 0:n], op0=ALU.mult, op1=ALU.mult),
                 r=[xk(h, k) for h in hbs] + ["std", "gains"], w=xkeys(k))

    def rstd(n):
        nb = n // 512
        P.op("act", lambda e: e.activation(out=STD[:, 0:n].rearrange("p (a b) -> p a b", a=nb), in_=PS[:, 6:6 + nb, :], func=AF.Ln,
                                          scale=1.0 / D, bias=EPS_AP),
             r=[("ps", 6 + i) for i in range(nb)] + ["cst"], w=["std"])
        P.op("act", lambda e: e.activation(out=STD[:, 0:n], in_=STD[:, 0:n], func=AF.Exp, scale=-0.5), r=["std"], w=["std"])

    wctr = {"i": 0, "o": 0}

    def ffn(b, w_in, w_out, gpre, gpost, dbg=None):
        c0 = b * 1024
        w_in_v = w_in.rearrange("(k p) f -> p k f", p=128)
        w_out_v = w_out.rearrange("(f p) m -> p f m", p=128)
        set_slots(4)
        prenorm(c0, 1024, gpre)
        pcnt = 0

        def enq_in(grp):
            bi = wctr["i"] % 2
            wctr["i"] += 1
            WGt = Wb[:, bi * 4096:bi * 4096 + 2048].rearrange("p (k f) -> p k f", k=8)
            WUt = Wb[:, bi * 4096 + 2048:bi * 4096 + 4096].rearrange("p (k f) -> p k f", k=8)
            kg = kW(bi * 4096, bi * 4096 + 2048)
            ku = kW(bi * 4096 + 2048, bi * 4096 + 4096)
            gsl = slice(0, 256) if TINY else slice(grp * 256, (grp + 1) * 256)
            usl = slice(256, 512) if TINY else slice(DFF + grp * 256, DFF + (grp + 1) * 256)
            enqueue(WGt, w_in_v[:, :, gsl], kg)
            enqueue(WUt, w_in_v[:, :, usl], ku)
            return WGt, WUt, kg, ku

        def enq_out(m):
            bo = wctr["o"] % 2
            wctr["o"] += 1
            WOt = Wb[:, 8192 + bo * 2816:8192 + (bo + 1) * 2816].rearrange("p (f m) -> p f m", f=22)
            ko = kW(8192 + bo * 2816, 8192 + (bo + 1) * 2816)
            enqueue(WOt, w_out_v[:, :, (slice(0, 128) if TINY else slice(m * 128, (m + 1) * 128))], ko)
            return WOt, ko

        ginfo = {0: enq_in(0)}
        cast_all()
        oinfo = {}
        for grp in range(11):
            if grp + 1 < 11:
                ginfo[grp + 1] = enq_in(grp + 1)
            else:
                oinfo[0] = enq_out(0)
            WGt, WUt, kg, ku = ginfo[grp]
            for hf in range(2):
                for fi in range(2):
                    f = grp * 2 + fi
                    gb = pcnt % 2
                    ub = 2 + pcnt % 2
                    s = pcnt % 2
                    pcnt += 1
                    for k in range(8):
                        P.op("pe", lambda e, k=k, gb=gb, fi=fi, hf=hf, WGt=WGt: e.matmul(
                            PS[:, gb, :], lhsT=WGt[:, k, fi * 128:(fi + 1) * 128], rhs=XN[:, k, hf * 512:(hf + 1) * 512],
                            start=(k == 0), stop=(k == 7)), r=kg + kXN(k), w=[("ps", gb)])
                    for k in range(8):
                        P.op("pe", lambda e, k=k, ub=ub, fi=fi, hf=hf, WUt=WUt: e.matmul(
                            PS[:, ub, :], lhsT=WUt[:, k, fi * 128:(fi + 1) * 128], rhs=XN[:, k, hf * 512:(hf + 1) * 512],
                            start=(k == 0), stop=(k == 7)), r=ku + kXN(k), w=[("ps", ub)])
                    P.op("act", lambda e, gb=gb, s=s: e.activation(out=SIL[s], in_=PS[:, gb, :], func=AF.Silu),
                         r=[("ps", gb)], w=[("sil", s)])
                    P.op("dve", lambda e, ub=ub, s=s, f=f, hf=hf: e.tensor_tensor(
                        out=Hh[:, f, hf * 512:(hf + 1) * 512], in0=SIL[s], in1=PS[:, ub, :], op=ALU.mult),
                        r=[("sil", s), ("ps", ub)], w=kHh(f, hf))
                    cast_some(2)
            cast_all()
        if dbg == 0.6:
            return
        for m in range(8):
            if m + 1 < 8:
                oinfo[m + 1] = enq_out(m + 1)
                cast_all()
            WOt, ko = oinfo[m]
            ob = [4, 0, 2][m % 3]
            for f in range(22):
                for hf in range(2):
                    P.op("pe", lambda e, f=f, hf=hf, ob=ob, WOt=WOt: e.matmul(
                        PS[:, ob + hf, :], lhsT=WOt[:, f, :], rhs=Hh[:, f, hf * 512:(hf + 1) * 512],
                        start=(f == 0), stop=(f == 21)), r=ko + kHh(f, hf), w=[("ps", ob + hf)])
            sq = SQ[m % 2]
            if dbg == 0.61:
                P.op("act", lambda e, ob=ob, m=m: e.activation(out=HO[:, m, :].rearrange("p (a b) -> p a b", a=2), in_=PS[:, ob:ob + 2, :],
                                                             func=AF.Copy), r=[("ps", ob), ("ps", ob + 1)], w=kHO(m))
                continue
            P.op("act", lambda e, ob=ob, sq=sq: e.activation(out=sq.rearrange("p (a b) -> p a b", a=2), in_=PS[:, ob:ob + 2, :],
                                                           func=AF.Square), r=[("ps", ob), ("ps", ob + 1)], w=[("sq", m % 2)])
            for hf in range(2):
                P.op("pe", lambda e, sq=sq, hf=hf, m=m: e.matmul(PS[:, 6 + hf, :], lhsT=ONESB[:], rhs=sq[:, hf * 512:(hf + 1) * 512],
                                                              start=(m == 0), stop=(m == 7)),
                     r=[("sq", m % 2), "onesb"], w=[("ps", 6 + hf)])
            P.op("dve", lambda e, ob=ob, m=m: e.tensor_scalar(out=HO[:, m, :].rearrange("p (a b) -> p a b", a=2), in0=PS[:, ob:ob + 2, :],
                                                            scalar1=GH[:, gpost, m:m + 1], scalar2=None, op0=ALU.mult),
                 r=[("ps", ob), ("ps", ob + 1), "gh", ("sq", m % 2)], w=kHO(m))
        if dbg in (0.61, 0.65):
            return
        rstd(1024)
        for m in range(8):
            P.op("dve", lambda e, m=m: e.tensor_tensor(out=HO[:, m, :], in0=HO[:, m, :], in1=STD[:, 0:1024], op=ALU.mult),
                 r=kHO(m) + ["std"], w=kHO(m))
            P.op("dve", lambda e, m=m: e.tensor_tensor(out=X[:, m, c0:c0 + 1024], in0=X[:, m, c0:c0 + 1024], in1=HO[:, m, :], op=ALU.add),
                 r=kHO(m) + [xk(2 * b, m), xk(2 * b + 1, m)], w=[xk(2 * b, m), xk(2 * b + 1, m)])
        set_slots(2)

    def store_out(b):
        for k in range(8):
            dma("sp", yT[:, k, b * 1024:(b + 1) * 1024], X[:, k, b * 1024:(b + 1) * 1024], [xk(2 * b, k), xk(2 * b + 1, k)],
                [("y", b, k)], "yo%d_%d" % (b, k))

    def finish():
        keys = list(P.lw.keys())
        P.op("sp", lambda e: None, r=[k for k in keys if isinstance(k, tuple) and k[0] == "y"])
        P.emit(nc, es)
        es.close()
        return nc

    if stage == 0:
        for b in range(2):
            store_out(b)
        return finish()
    if stage == 0.5:
        prenorm(0, 1024, 0)
        for b in range(2):
            store_out(b)
        return finish()
    for b in range(2):
        ffn(b, w1i, w1o, 0, 1, stage)
        if stage in (0.6, 0.61, 0.65, 0.7):
            break
    if stage <= 1:
        for b in range(2):
            store_out(b)
        return finish()

    wmi_v = wmi.rearrange("(k p) f -> p k f", p=128)

    def wtile(lo, n, kdim=8):
        return Wb[:, lo:lo + n].rearrange("p (k f) -> p k f", k=kdim), kW(lo, lo + n)

    WQK, kWQK = wtile(0, 4096)
    WV, kWV = wtile(4096, 4096)
    WOG, kWOG = wtile(8192, 4096)
    WGA, kWGA = wtile(12288, 128)
    WBt = [wtile(12544 + i * 1024, 1024) for i in range(2)]
    XB, kXB = wtile(14592, 32)
    GROW = ROWP[:, 0:512]
    BI = bc(ROWP[:, 512:520], [[0, 16], [1, 8]])
    BF_ = bc(ROWP[:, 520:528], [[0, 16], [1, 8]])
    gk = ["gates"]
    RQf = Wb[:, 13568:14080]; kRQ = kW(13568, 14080)
    RQ = RQf.rearrange("p (a b) -> p a b", a=2)
    SBFf = Wb[:, 14080:15112]; kSBF = kW(14080, 15112)
    SBF = SBFf.rearrange("p (a b) -> p a b", a=4)

    def ksbf(i):
        return kSBF

    def ftile(lo, n):
        return Wf[:, lo:lo + n], kW(2 * lo, 2 * (lo + n))

    def btile(lo, n):
        return Wb[:, lo:lo + n], kW(lo, lo + n)

    E_t, kE = ftile(0, 512)
    L1f, kL1 = ftile(512, 512); L1 = L1f.rearrange("p (h s) -> p h s", h=4)
    L2f, kL2 = ftile(1024, 512); L2 = L2f.rearrange("p (h s) -> p h s", h=4)
    TIf, kTI = ftile(1536, 516); TI = TIf.rearrange("p (h v) -> p h v", h=4)
    NDf, kND = ftile(2052, 516); ND = NDf.rearrange("p (h v) -> p h v", h=4)
    HHf, kHH = ftile(2568, 512); HHt = HHf.rearrange("p (h v) -> p h v", h=4)
    SQT, kSQT = ftile(3080, 512)
    PT, kPT = btile(7184, 512)
    KT2f, kKT2 = btile(7696, 256); KT2 = KT2f.rearrange("p (h k) -> p h k", h=4)
    YTOK, kYTOK = btile(8208, 512)
    PT_b, kPT_b = btile(15360, 512)
    KT2f_b, kKT2_b = btile(15872, 256); KT2_b = KT2f_b.rearrange("p (h k) -> p h k", h=4)
    PTs = [(PT, kPT), (PT_b, kPT_b)]
    KT2s = [(KT2, kKT2), (KT2_b, kKT2_b)]
    KTAs = [btile(9232 + i * 512, 512) for i in range(2)]
    EXS, kEXS = ftile(5128, 524)
    SSTs = [[ftile(4736 + (d * 2 + p) * 256, 129) for p in range(2)] for d in range(2)]
    ACCS, kACCS = ftile(6200, 524)
    EXG = R1[:, 0:4192].rearrange("p (r c) -> p r c", r=8)
    kEXG = kR(0, 4192)

    def ktok_transposes(c, bank):
        cs = slice(c * 128, (c + 1) * 128)
        kb = PS[:, bank, 0:128].bitcast(BF16)
        for p in range(2):
            P.op("pe", lambda e, p=p, kb=kb, cs=cs: e.transpose(out=kb[:, p * 128:(p + 1) * 128], in_=QKT[:, 2 + p, cs], identity=IDENTB[:]),
                 r=kQK(2 + p, c * 128, (c + 1) * 128) + ["identb"], w=[("ps", bank)])
        return kb


    def mixer_front(do_pass1):
        set_slots(4 if do_pass1 else 2)
        enqueue(WQK, wmi_v[:, :, 1536:2048], kWQK)
        enqueue(WV, wmi_v[:, :, 2048:2560], kWV)
        if not do_pass1:
            enqueue(WOG, wmi_v[:, :, 2560:3072], kWOG)
        enqueue(WGA, wmi_v[:, :, 3072:3088], kWGA)
        P.op("dve", lambda e: e.memset(VAUG[:, :, :, 128:129], 1.0), w=[k for c in range(16) for k in kVA(c)])
        cnt2 = 0
        wbc = 0
        def enq_wb(m8):
            nonlocal wbc
            WBw, kWBw = WBt[wbc % 2]
            wbc += 1
            enqueue(WBw, wmi_v[:, :, 512 + m8 * 128:512 + (m8 + 1) * 128], kWBw)
            return WBw, kWBw

        XNs = [XN, R1b[:, 8192:16384].rearrange("p (k t) -> p k t", k=8)]
        kXNs = [lambda k: kXN(k), lambda k: kR(4096 + k * 512, 4096 + (k + 1) * 512)]
        prenorm(0, 1024, 2, XNs[0], kXNs[0])
        cast_all()
        prenorm(1024, 1024, 2, XNs[1], kXNs[1])
        if do_pass1:
            load_x(xT, "pool")
        for b in range(2):
            XNb, kXNb = XNs[b], kXNs[b]
            wbi = {0: enq_wb(0)}
            for m in range(2 if do_pass1 else 0, 4):
                for hf in range(2):
                    pb = cnt2 % 2
                    cnt2 += 1
                    for k in range(8):
                        P.op("pe", lambda e, XNb=XNb, k=k, m=m, hf=hf, pb=pb: e.matmul(PS[:, pb, :], lhsT=WQK[:, k, m * 128:(m + 1) * 128],
                                                                           rhs=XNb[:, k, hf * 512:(hf + 1) * 512], start=(k == 0), stop=(k == 7)),
                             r=kWQK + kXNb(k), w=[("ps", pb)])
                    cs = b * 1024 + hf * 512
                    P.op("act", lambda e, m=m, pb=pb, cs=cs: e.activation(out=QKT[:, m, cs:cs + 512], in_=PS[:, pb, :], func=AF.Copy,
                                                                         scale=(0.125 if m < 2 else 1.0)),
                         r=[("ps", pb)], w=kQK(m, cs, cs + 512))
            for cc in range(8):
                c = b * 8 + cc
                tc_ = slice(cc * 128, (cc + 1) * 128)
                vb = 2 + cc % 2
                obk = 4 + cc % 2
                for k in range(8):
                    P.op("pe", lambda e, XNb=XNb, k=k, tc_=tc_, vb=vb: e.matmul(PS[:, vb, :], lhsT=XNb[:, k, tc_], rhs=WV[:, k, :], start=(k == 0), stop=(k == 7)),
                         r=kWV + kXNb(k), w=[("ps", vb)])
                P.op("dve", lambda e, c=c, vb=vb: e.tensor_copy(out=VAUG[:, c, :, 0:128], in_=PS[:, vb, :].rearrange("p (h v) -> p h v", h=4)),
                     r=[("ps", vb)], w=kVA(c))
                if not do_pass1:
                    for k in range(8):
                        P.op("pe", lambda e, XNb=XNb, k=k, tc_=tc_, obk=obk: e.matmul(PS[:, obk, :], lhsT=XNb[:, k, tc_], rhs=WOG[:, k, :], start=(k == 0), stop=(k == 7)),
                             r=kWOG + kXNb(k), w=[("ps", obk)])
                    P.op("act", lambda e, obk=obk: e.activation(out=SIL[0], in_=PS[:, obk, :], func=AF.Sigmoid), r=[("ps", obk)], w=[("sil", 0)])
                    P.op("dve", lambda e, c=c: e.tensor_tensor(out=SGO[:, c, :], in0=SIL[0], in1=GROW, op=ALU.mult),
                         r=[("sil", 0), "rowp"], w=kSGO(c))
                for k in range(8):
                    P.op("pe", lambda e, XNb=XNb, k=k, tc_=tc_, cc=cc: e.matmul(PS[:, 7, cc * 16:(cc + 1) * 16], lhsT=XNb[:, k, tc_], rhs=WGA[:, k, :],
                                                                    start=(k == 0), stop=(k == 7)),
                         r=kWGA + kXNb(k), w=[("ps", 7)])
            P.op("dve", lambda e, XNb=XNb, b=b: e.tensor_copy(out=GR[:, b * 8:(b + 1) * 8, :], in_=PS[:, 7, 0:128].rearrange("p (c g) -> p c g", c=8)),
                 r=[("ps", 7)], w=["gr"])
            allxn = [k_ for k in range(8) for k_ in kXNb(k)]
            P.op("dve", lambda e, XNb=XNb: e.tensor_copy(out=XB[:, :, 0:2], in_=bc(XNb[:, :, 0:1], [[1024, 8], [512, 2]])), r=allxn, w=kXB)
            P.op("dve", lambda e, XNb=XNb: e.tensor_copy(out=XB[:, :, 2:4], in_=bc(XNb[:, :, 511:512], [[1024, 8], [512, 2]])), r=allxn, w=kXB)
            for m8 in range(8):
                cast_all()
                if m8 + 1 < 8:
                    wbi[m8 + 1] = enq_wb(m8 + 1)
                WBw, kWBw = wbi[m8]
                for k in range(8):
                    P.op("pe", lambda e, k=k, m8=m8, WBw=WBw: e.matmul(PS[:, 6, m8 * 4:(m8 + 1) * 4], lhsT=WBw[:, k, :], rhs=XB[:, k, :],
                                                                    start=(k == 0), stop=(k == 7)),
                         r=kWBw + kXB, w=[("ps", 6)])
            P.op("act", lambda e: e.activation(out=UBT[:, 0:16], in_=PS[:, 6, 0:16], func=AF.Copy), r=[("ps", 6)], w=["ubt"])
            P.op("dve", lambda e, b=b: e.tensor_tensor(out=UB[:, :, b * 4:(b + 1) * 4], in0=UBT[:, 0:16].rearrange("p (c t) -> p c t", c=4),
                                                     in1=PS[:, 6, 16:32].rearrange("p (c t) -> p c t", c=4), op=ALU.mult),
                 r=["ubt", ("ps", 6)], w=["ub"])

        P.op("dve", lambda e: e.tensor_tensor(out=LI, in0=GR[:, :, 0:8], in1=BI, op=ALU.add), r=["gr", "rowp"], w=gk)
        P.op("dve", lambda e: e.tensor_tensor(out=ZZ, in0=GR[:, :, 8:16], in1=BF_, op=ALU.add), r=["gr", "rowp"], w=gk)
        P.op("act", lambda e: e.activation(out=AZ, in_=ZZ, func=AF.Abs), r=gk, w=gk)
        P.op("act", lambda e: e.activation(out=AZ, in_=AZ, func=AF.Exp, scale=-1.0), r=gk, w=gk)
        P.op("act", lambda e: e.activation(out=AZ, in_=AZ, func=AF.Ln, bias=ONE_AP), r=gk + ["cst"], w=gk)
        P.op("dve", lambda e: e.tensor_scalar(out=ZZ, in0=ZZ, scalar1=0.0, scalar2=None, op0=ALU.min), r=gk, w=gk)
        P.op("dve", lambda e: e.tensor_tensor(out=LF, in0=ZZ, in1=AZ, op=ALU.subtract), r=gk, w=gk)
        P.op("dve", lambda e: e.tensor_scalar(out=LIB, in0=LI, scalar1=BIG, scalar2=None, op0=ALU.add), r=gk, w=gk)
        P.op("pe", lambda e: e.matmul(PS[:, 0, 0:64], lhsT=MASKS[:, 0, :], rhs=LF[:, :, 0:4], start=True, stop=True),
             r=gk + ["masks"], w=[("ps", 0)])
        P.op("pe", lambda e: e.matmul(PS[:, 0, 64:128], lhsT=MASKS[:, 1, :], rhs=LF[:, :, 4:8], start=True, stop=True),
             r=gk + ["masks"], w=[("ps", 0)])
        P.op("pe", lambda e: e.matmul(PS[:, 0, 128:256], lhsT=ONESF[:], rhs=LF.rearrange("p c j -> p (c j)"), start=True, stop=True),
             r=gk + ["onesf"], w=[("ps", 0)])
        P.op("dve", lambda e: e.tensor_copy(out=BC[:, :, 0:4], in_=PS[:, 0, 0:64].rearrange("p (c j) -> p c j", c=16)), r=[("ps", 0)], w=gk)
        P.op("dve", lambda e: e.tensor_copy(out=BC[:, :, 4:8], in_=PS[:, 0, 64:128].rearrange("p (c j) -> p c j", c=16)), r=[("ps", 0)], w=gk)
        P.op("dve", lambda e: e.tensor_copy(out=TOT, in_=PS[:, 0, 128:256].rearrange("p (c j) -> p c j", c=16)), r=[("ps", 0)], w=gk)
        P.op("dve", lambda e: e.tensor_tensor(out=EW, in0=TOT, in1=BC, op=ALU.subtract), r=gk, w=gk)
        P.op("dve", lambda e: e.tensor_tensor(out=EW, in0=EW, in1=LI, op=ALU.add), r=gk, w=gk)
        P.op("dve", lambda e: e.memset(GAF, 0.0), r=gk, w=gk)
        for c in range(14, -1, -1):
            P.op("dve", lambda e, c=c: e.tensor_tensor(out=GAF[:, c, 0:4], in0=GAF[:, c + 1, 0:4], in1=TOT[:, c + 1, 0:4], op=ALU.add), r=gk, w=gk)
        for c in range(1, 16):
            P.op("dve", lambda e, c=c: e.tensor_tensor(out=GAF[:, c, 4:8], in0=GAF[:, c - 1, 4:8], in1=TOT[:, c - 1, 4:8], op=ALU.add), r=gk, w=gk)
        P.op("dve", lambda e: e.tensor_tensor(out=EWA, in0=EW, in1=GAF, op=ALU.add), r=gk, w=gk)
        P.op("act", lambda e: e.activation(out=EWA, in_=EWA, func=AF.Exp), r=gk, w=gk)
        P.op("act", lambda e: e.activation(out=EW, in_=EW, func=AF.Exp), r=gk, w=gk)
        P.op("act", lambda e: e.activation(out=EB, in_=BC, func=AF.Exp), r=gk, w=gk)
        P.op("act", lambda e: e.activation(out=EGt, in_=TOT, func=AF.Exp), r=gk, w=gk)

        if do_pass1:
            for c in range(16):
                bank = 4 + c % 2
                kb = ktok_transposes(c, bank)
                KTAf, kKTA = KTAs[c % 2]
                KTA = KTAf.rearrange("p (d h k) -> p d h k", d=2, h=4)
                P.op("dve", lambda e, KTA=KTA, kb=kb, c=c: e.tensor_tensor(
                    out=KTA, in0=bc(kb, [[0, 2], [64, 4], [1, 64]]), in1=bc(EWA[:, c, :], [[4, 2], [1, 4], [0, 64]]), op=ALU.mult),
                    r=[("ps", bank)] + gk, w=kKTA)
                for d in range(2):
                    for p in range(2):
                        P.op("pe", lambda e, KTA=KTA, d=d, p=p, c=c: e.matmul(
                            PS[:, d * 2 + p, 0:258], lhsT=KTA[:, d, 2 * p:2 * p + 2, :].rearrange("p h k -> p (h k)"),
                            rhs=VAUG[:, c, 2 * p:2 * p + 2, :].rearrange("p h v -> p (h v)"), start=(c == 0), stop=(c == 15)),
                            r=kKTA + kVA(c), w=[("ps", d * 2 + p)])
            for d in range(2):
                for p in range(2):
                    i = d * 2 + p
                    P.op("dve", lambda e, i=i: e.tensor_copy(out=EXS[0:64, i * 129:(i + 1) * 129], in_=PS[0:64, i, 0:129]), r=[("ps", i)], w=kEXS)
                    P.op("dve", lambda e, i=i: e.tensor_copy(out=EXS[64:128, i * 129:(i + 1) * 129], in_=PS[64:128, i, 129:258]), r=[("ps", i)], w=kEXS)
            P.op("dve", lambda e: e.tensor_copy(out=EXS[:, 516:520], in_=UB[:, :, 0]), r=["ub"], w=kEXS)
            P.op("dve", lambda e: e.tensor_copy(out=EXS[:, 520:524], in_=UB[:, :, 7]), r=["ub"], w=kEXS)

    mixer_front(True)
    dma("sp", pay_d, EXS, kEXS, ["pay"], "payo")
    for b in range(2):
        ffn(b, w1i, w1o, 0, 1)
    mixer_front(False)
    P.op("dve", lambda e: e.memset(SBFf, 0.0), w=kSBF)
    P.op("dve", lambda e: e.memset(RQf, 0.0), w=kRQ)
    dma("sp", ACCS, pay_d, ["pay"], kACCS, "payi")
    for (lo, hi, so) in [(0, 258, 0), (520, 524, 0), (258, 520, 1)]:
        P.op("dve", lambda e, lo=lo, hi=hi, so=so: e.tensor_scalar(out=ACCS[:, lo:hi], in0=ACCS[:, lo:hi], scalar1=SEL[:, so:so + 1],
                                                              scalar2=None, op0=ALU.mult), r=kACCS + ["sel"], w=kACCS)
    for d in range(2):
        for p in range(2):
            i = d * 2 + p
            SSt, kSS = SSTs[d][p]
            P.op("dve", lambda e, SSt=SSt, i=i: e.tensor_copy(out=SSt, in_=ACCS[:, i * 129:(i + 1) * 129]), r=kACCS, w=kSS)
            P.op("act", lambda e, i=i: e.activation(out=SBF[0:64, i, 0:129], in_=ACCS[0:64, i * 129:(i + 1) * 129], func=AF.Copy),
                 r=kACCS, w=ksbf(i))
            P.op("act", lambda e, i=i: e.activation(out=SBF[64:128, i, 129:258], in_=ACCS[64:128, i * 129:(i + 1) * 129], func=AF.Copy),
                 r=kACCS, w=ksbf(i))
    P.op("dve", lambda e: e.tensor_copy(out=HALO[:, :, 0], in_=ACCS[:, 520:524]), r=kACCS, w=["halo"])
    P.op("dve", lambda e: e.tensor_copy(out=HALO[:, :, 1], in_=ACCS[:, 516:520]), r=kACCS, w=["halo"])

    HS = R1[:].rearrange("p (c f) -> p c f", c=16)
    arrived = [0] * 16

    def kHS(c):
        return kR(c * 512, (c + 1) * 512)

    def fin_a(c):
        hsf = HS[:, c, :]
        P.op("dve", lambda e: e.tensor_tensor(out=SQT, in0=hsf, in1=hsf, op=ALU.mult), r=kHS(c), w=kSQT)
        P.op("dve", lambda e: e.tensor_reduce(out=SSH, in_=SQT.rearrange("p (h v) -> p h v", h=4), axis=AX.X, op=ALU.add), r=kSQT, w=["ssh"])
        P.op("act", lambda e: e.activation(out=SSH, in_=SSH, func=AF.Ln, scale=1.0 / 128, bias=EPS_AP), r=["ssh", "cst"], w=["ssh"])
        P.op("act", lambda e: e.activation(out=SSH, in_=SSH, func=AF.Exp, scale=-0.5), r=["ssh"], w=["ssh"])
        P.op("dve", lambda e: e.tensor_tensor(out=SQT.rearrange("p (h v) -> p h v", h=4), in0=hsf.rearrange("p (h v) -> p h v", h=4),
                                             in1=bc(SSH, [[1, 4], [0, 128]]), op=ALU.mult), r=kHS(c) + ["ssh"], w=kSQT)
        P.op("dve", lambda e: e.tensor_tensor(out=YTOK, in0=SQT, in1=SGO[:, c, :], op=ALU.mult), r=kSQT + kSGO(c), w=kYTOK)

    def fin_b(c):
        cs = slice(c * 128, (c + 1) * 128)
        yb = PS[:, 1, 0:256].bitcast(BF16)
        for ft in range(4):
            P.op("pe", lambda e, ft=ft: e.transpose(out=yb[:, ft * 128:(ft + 1) * 128], in_=YTOK[:, ft * 128:(ft + 1) * 128], identity=IDENTB[:]),
                 r=kYTOK + ["identb"], w=[("ps", 1)])
        P.op("act", lambda e: e.activation(out=QKT[:, :, cs], in_=yb.rearrange("p (f t) -> p f t", f=4), func=AF.Copy),
             r=[("ps", 1)], w=kQKall(c * 128, (c + 1) * 128))

    pending_fin = []

    def stepA(c, d, par):
        cs = slice(c * 128, (c + 1) * 128)
        c0_, c1_ = c * 128, (c + 1) * 128
        j0 = d * 4
        PT, kPT = PTs[par]
        KT2, kKT2 = KT2s[par]
        msu = MASKS[:, 2 + d, :]
        P.op("dve", lambda e: e.tensor_tensor(out=L1, in0=bc(msu, [[0, 4], [1, 128]]), in1=bc(LF[:, c, j0:j0 + 4], [[1, 4], [0, 128]]), op=ALU.mult),
             r=gk + ["masks"], w=kL1)
        P.op("dve", lambda e: e.tensor_tensor(out=L2, in0=bc(MASKS[:, 4, :], [[0, 4], [1, 128]]), in1=bc(LIB[:, c, j0:j0 + 4], [[1, 4], [0, 128]]), op=ALU.mult),
             r=gk + ["masks"], w=kL2)
        P.op("dve", lambda e: e.tensor_tensor(out=L1, in0=L1, in1=L2, op=ALU.add), r=kL1 + kL2, w=kL1)
        yield
        for p in range(2):
            P.op("act", lambda e, p=p: e.activation(out=RQ[0:64, p, 0:128], in_=QKT[0:64, p, cs], func=AF.Copy),
                 r=kQK(p, c0_, c1_), w=kRQ)
            P.op("act", lambda e, p=p: e.activation(out=RQ[64:128, p, 128:256], in_=QKT[64:128, p, cs], func=AF.Copy),
                 r=kQK(p, c0_, c1_), w=kRQ)
        yield
        kb = ktok_transposes(c, 5)
        for p in range(2):
            P.op("pe", lambda e, p=p: e.matmul(PS[:, 0, p * 256:(p + 1) * 256], lhsT=QKT[:, 2 + p, cs], rhs=RQ[:, p, :], start=True, stop=True),
                 r=kQK(2 + p, c0_, c1_) + kRQ, w=[("ps", 0)])
        for h in range(4):
            P.op("pe", lambda e, h=h: e.matmul(PS[:, 1, h * 128:(h + 1) * 128], lhsT=L1[:, h, :], rhs=MASKS[:, d, :], start=True, stop=True),
                 r=kL1 + ["masks"], w=[("ps", 1)])
        yield
        P.op("act", lambda e: e.activation(out=E_t, in_=PS[:, 1, :], func=AF.Exp, bias=NBIG_AP), r=[("ps", 1), "cst"], w=kE)
        yield
        P.op("dve", lambda e: e.tensor_tensor(out=PT, in0=E_t, in1=PS[:, 0, :], op=ALU.mult), r=kE + [("ps", 0)], w=kPT)
        P.op("dve", lambda e: e.tensor_tensor(out=KT2, in0=bc(kb, [[64, 4], [1, 64]]), in1=bc(EW[:, c, j0:j0 + 4], [[1, 4], [0, 64]]), op=ALU.mult),
             r=[("ps", 5)] + gk, w=kKT2)
        yield

    def stepB(c, d, par):
        cs = slice(c * 128, (c + 1) * 128)
        c0_, c1_ = c * 128, (c + 1) * 128
        j0 = d * 4
        PT, kPT = PTs[par]
        KT2, kKT2 = KT2s[par]
        for h in range(4):
            outp = PS[:, 4, h * 129:(h + 1) * 129] if h < 3 else PS[:, 7, 258:387]
            P.op("pe", lambda e, h=h, outp=outp: e.matmul(outp, lhsT=PT[:, h * 128:(h + 1) * 128], rhs=VAUG[:, c, h, :], start=True, stop=True),
                 r=kPT + kVA(c), w=[("ps", 4 if h < 3 else 7)])
        regs = [PS[:, 2, 0:258], PS[:, 3, 0:258]]
        for p in range(2):
            P.op("pe", lambda e, p=p: e.matmul(regs[p], lhsT=KT2[:, 2 * p:2 * p + 2, :].rearrange("p h k -> p (h k)"),
                                             rhs=VAUG[:, c, 2 * p:2 * p + 2, :].rearrange("p h v -> p (h v)"), start=True, stop=True),
                 r=kKT2 + kVA(c), w=[("ps", 2 + p)])
        for p in range(2):
            P.op("pe", lambda e, p=p: e.matmul(PS[:, 6 + p, 0:258], lhsT=QKT[:, p, cs], rhs=SBF[:, d * 2 + p, :], start=True, stop=True),
                 r=kQK(p, c0_, c1_) + ksbf(d * 2 + p), w=[("ps", 6 + p)])
        yield
        for h in range(4):
            inp = PS[:, 6 + h // 2, (h % 2) * 129:(h % 2 + 1) * 129]
            P.op("act", lambda e, h=h, inp=inp: e.activation(out=TI[:, h, :], in_=inp, func=AF.Copy, scale=EB[:, c, j0 + h:j0 + h + 1]),
                 r=[("ps", 6 + h // 2)] + gk, w=kTI)
        yield
        P.op("dve", lambda e: e.tensor_tensor(out=ND[:, 0:3, :], in0=TI[:, 0:3, :], in1=PS[:, 4, 0:387].rearrange("p (h v) -> p h v", h=3), op=ALU.add),
             r=kTI + [("ps", 4)], w=kND)
        P.op("dve", lambda e: e.tensor_tensor(out=ND[:, 3, :], in0=TI[:, 3, :], in1=PS[:, 7, 258:387], op=ALU.add), r=kTI + [("ps", 7)], w=kND)
        P.op("act", lambda e: e.activation(out=DEN, in_=ND[:, :, 128], func=AF.Abs), r=kND, w=["den"])
        P.op("dve", lambda e: e.tensor_scalar(out=DEN, in0=DEN, scalar1=1.0, scalar2=None, op0=ALU.max), r=["den"], w=["den"])
        P.op("dve", lambda e: e.reciprocal(out=RD, in_=DEN), r=["den"], w=["rd"])
        rdb = bc(RD, [[1, 4], [0, 128]])
        hsv = HS[:, c, :].rearrange("p (h v) -> p h v", h=4)
        if arrived[c] == 0:
            P.op("dve", lambda e: e.tensor_tensor(out=hsv, in0=ND[:, :, 0:128], in1=rdb, op=ALU.mult), r=kND + ["rd"], w=kHS(c))
        else:
            P.op("dve", lambda e: e.tensor_tensor(out=HHt, in0=ND[:, :, 0:128], in1=rdb, op=ALU.mult), r=kND + ["rd"], w=kHH)
            P.op("dve", lambda e: e.tensor_tensor(out=hsv, in0=hsv, in1=HHt, op=ALU.add), r=kHH + kHS(c), w=kHS(c))
        arrived[c] += 1
        yield
        for p in range(2):
            reg = regs[p]
            bk = 2 + p
            i = d * 2 + p
            SSt, kSS = SSTs[d][p]
            for half in range(2):
                r0 = half * 64
                j = j0 + 2 * p + half
                P.op("dve", lambda e, SSt=SSt, reg=reg, r0=r0, j=j, half=half: e.scalar_tensor_tensor(
                    out=SSt[r0:r0 + 64, :], in0=SSt[r0:r0 + 64, :], scalar=EGt[r0:r0 + 64, c, j:j + 1],
                    in1=reg[r0:r0 + 64, half * 129:(half + 1) * 129], op0=ALU.mult, op1=ALU.add),
                    r=[("ps", bk)] + kSS + gk, w=kSS)
            P.op("act", lambda e, SSt=SSt, i=i: e.activation(out=SBF[0:64, i, 0:129], in_=SSt[0:64, :], func=AF.Copy), r=kSS, w=ksbf(i))
            P.op("act", lambda e, SSt=SSt, i=i: e.activation(out=SBF[64:128, i, 129:258], in_=SSt[64:128, :], func=AF.Copy), r=kSS, w=ksbf(i))
        if arrived[c] == 2:
            fin_a(c)
            pending_fin.append(c)
        yield

    seq = []
    for i in range(16):
        seq += [(i, 0), (15 - i, 1)]
    for _ in stepA(*seq[0], 0):
        pass
    for n in range(32):
        gB = stepB(*seq[n], n % 2)
        gA = stepA(*seq[n + 1], (n + 1) % 2) if n + 1 < 32 else None
        next(gB)
        if gA: next(gA)
        next(gB)
        if gA: next(gA)
        if gA: next(gA)
        next(gB)
        if gA: next(gA)
        next(gB)
        if pending_fin:
            fin_b(pending_fin.pop(0))
        if gA: next(gA)
    while pending_fin:
        fin_b(pending_fin.pop(0))

    WCs = [wtile(i * 4096, 4096) for i in range(2)]
    WMO, kWMO = wtile(8192, 8192)
    wmo_v = wmo.rearrange("(k p) f -> p k f", p=128)
    set_slots(4)
    c1seq = [(hb, g) for hb in range(4) for g in range(3)]

    def enq_c(idx):
        WCw, kWCw = WCs[idx % 2]
        enqueue(WCw, wmi_v[:, :, c1seq[idx][1] * 512:(c1seq[idx][1] + 1) * 512], kWCw)
        return WCw, kWCw

    c1info = {0: enq_c(0)}
    enqueue(WMO, wmo_v, kWMO)
    MO = R1[:, 4096:8192].rearrange("p (m t) -> p m t", m=8)

    def kMO(m):
        return kR(4096 + m * 512, 4096 + (m + 1) * 512)
    ubidx = {0: (None, 1), 1: (2, 4), 2: (3, 5), 3: (6, None)}
    wcc = 0
    pcc = 0
    XNc = [XN[:, :, 0:512], XN[:, :, 512:1024]]

    def prenorm_c1(hb):
        par = hb % 2
        prenorm(hb * 512, 512, 2, XNc[par], lambda k: kXN(k, 1024) + [("xnc", par, k)])

    prenorm_c1(0)
    for hb in range(4):
        c0 = hb * 512
        par = hb % 2
        li_, ri_ = ubidx[hb]
        lsrc = HALO[:, :, 0] if li_ is None else UB[:, :, li_]
        rsrc = HALO[:, :, 1] if ri_ is None else UB[:, :, ri_]
        allU = [k_ for i in range(4) for k_ in kU(i)]
        P.op("dve", lambda e, lsrc=lsrc: e.tensor_copy(out=U[:, :, 0], in_=lsrc), r=["halo", "ub"], w=allU)
        P.op("dve", lambda e, rsrc=rsrc: e.tensor_copy(out=U[:, :, 513], in_=rsrc), r=["halo", "ub"], w=allU)

        def conv_ct(ct):
            P.op("dve", lambda e: e.tensor_scalar(out=CTMP, in0=U[:, ct, 0:512], scalar1=CONVP[:, ct, 0:1], scalar2=None, op0=ALU.mult),
                 r=kU(ct) + ["convp"], w=kCTMP)
            P.op("dve", lambda e: e.scalar_tensor_tensor(out=CTMP, in0=U[:, ct, 1:513], scalar=CONVP[:, ct, 1:2], in1=CTMP, op0=ALU.mult, op1=ALU.add),
                 r=kU(ct) + ["convp"] + kCTMP, w=kCTMP)
            P.op("dve", lambda e: e.scalar_tensor_tensor(out=CTMP, in0=U[:, ct, 2:514], scalar=CONVP[:, ct, 2:3], in1=CTMP, op0=ALU.mult, op1=ALU.add),
                 r=kU(ct) + ["convp"] + kCTMP, w=kCTMP)
            P.op("dve", lambda e: e.scalar_tensor_tensor(out=YC[:, ct, :], in0=CTMP, scalar=CONVP[:, ct, 3:4], in1=BG[:, ct, :], op0=ALU.add, op1=ALU.mult),
                 r=kCTMP + kBG(ct) + ["convp"], w=kYC(ct))

        for g in range(3):
            idx = hb * 3 + g
            cast_all()
            if idx + 1 < 12:
                c1info[idx + 1] = enq_c(idx + 1)
            WCw, kWCw = c1info[idx]
            for mt in range(4):
                pb = pcc % 2
                pcc += 1
                for k in range(8):
                    P.op("pe", lambda e, k=k, WCw=WCw, mt=mt, pb=pb, par=par: e.matmul(PS[:, pb, :], lhsT=WCw[:, k, mt * 128:(mt + 1) * 128], rhs=XNc[par][:, k, :],
                                                                           start=(k == 0), stop=(k == 7)),
                         r=kWCw + [("xnc", par, k)], w=[("ps", pb)])
                if g == 0:
                    P.op("act", lambda e, mt=mt, pb=pb: e.activation(out=BG[:, mt, :], in_=PS[:, pb, :], func=AF.Copy), r=[("ps", pb)], w=kBG(mt))
                elif g == 1:
                    P.op("act", lambda e, mt=mt, pb=pb: e.activation(out=U[:, mt, 1:513], in_=PS[:, pb, :], func=AF.Copy), r=[("ps", pb)], w=kU(mt))
                else:
                    P.op("dve", lambda e, mt=mt, pb=pb: e.tensor_tensor(out=U[:, mt, 1:513], in0=U[:, mt, 1:513], in1=PS[:, pb, :], op=ALU.mult),
                         r=[("ps", pb)] + kU(mt), w=kU(mt))
                    conv_ct(mt)
                cast_some(2)
        if hb + 1 < 4:
            prenorm_c1(hb + 1)
        for m in range(8):
            pb = 2 + m % 2
            for k in range(8):
                rhs = YC[:, k, :] if k < 4 else QKT[:, k - 4, c0:c0 + 512]
                rk = kYC(k) if k < 4 else kQK(k - 4, c0, c0 + 512)
                P.op("pe", lambda e, k=k, m=m, pb=pb, rhs=rhs: e.matmul(PS[:, pb, :], lhsT=WMO[:, k, m * 128:(m + 1) * 128], rhs=rhs, start=(k == 0), stop=(k == 7)),
                     r=kWMO + rk, w=[("ps", pb)])
            sq = SQ[m % 2][:, 0:512]
            P.op("act", lambda e, pb=pb, sq=sq: e.activation(out=sq, in_=PS[:, pb, :], func=AF.Square), r=[("ps", pb)], w=[("sq", m % 2)])
            P.op("pe", lambda e, sq=sq, m=m: e.matmul(PS[:, 6, :], lhsT=ONESB[:], rhs=sq, start=(m == 0), stop=(m == 7)),
                 r=[("sq", m % 2), "onesb"], w=[("ps", 6)])
            P.op("dve", lambda e, pb=pb, m=m: e.tensor_scalar(out=MO[:, m, :], in0=PS[:, pb, :], scalar1=GH[:, 3, m:m + 1], scalar2=None, op0=ALU.mult),
                 r=[("ps", pb), "gh", ("sq", m % 2)], w=kMO(m))
        rstd(512)
        for m in range(8):
            P.op("dve", lambda e, m=m: e.tensor_tensor(out=MO[:, m, :], in0=MO[:, m, :], in1=STD[:, 0:512], op=ALU.mult),
                 r=kMO(m) + ["std"], w=kMO(m))
            P.op("dve", lambda e, m=m, c0=c0: e.tensor_tensor(out=X[:, m, c0:c0 + 512], in0=X[:, m, c0:c0 + 512], in1=MO[:, m, :], op=ALU.add),
                 r=kMO(m) + [xk(hb, m)], w=[xk(hb, m)])
    if stage == 2:
        for b in range(2):
            store_out(b)
        return finish()

    for b in range(2):
        ffn(b, w2i, w2o, 4, 5)
        store_out(b)
    return finish()


_CACHE = {}


def _prep(inputs, stage):
    f = lambda a: np.ascontiguousarray(np.asarray(a, dtype=np.float32))
    x = f(inputs["x"])
    gains = np.stack([f(inputs[n])[0] for n in ["norm_ffn1_pre", "norm_ffn1_post", "norm_mix_pre", "norm_mix_post",
                                                 "norm_ffn2_pre", "norm_ffn2_post"]], 0)
    gains = np.ascontiguousarray(gains.reshape(6, 8, 128).transpose(2, 0, 1))
    cw = f(inputs["conv_w"])[0]
    cb = f(inputs["conv_b"])[0]
    convp = np.concatenate([cw, cb[None, :]], 0)
    convp = np.ascontiguousarray(convp.reshape(4, 4, 128).transpose(2, 1, 0))
    row = np.concatenate([f(inputs["mlstm_norm"])[0], f(inputs["gate_i_bias"])[0], f(inputs["gate_f_bias"])[0]])
    rowp = np.ascontiguousarray(np.broadcast_to(row[None, :], (128, 528)))
    r = np.arange(128)
    masks = np.stack([(r[:, None] <= r[None, :]), (r[:, None] >= r[None, :]), (r[:, None] > r[None, :]),
                      (r[:, None] < r[None, :]), (r[:, None] == r[None, :])], 1).astype(np.float32)
    identb = np.eye(128, dtype=np.float32).astype(ml_dtypes.bfloat16)
    common = {
        "w1i": (f(inputs["w_ffn1_in"])[0][:, :512].copy() if os.environ.get("KTINY") else f(inputs["w_ffn1_in"])[0]),
        "w1o": (f(inputs["w_ffn1_out"])[0][:, :128].copy() if os.environ.get("KTINY") else f(inputs["w_ffn1_out"])[0]),
        "wmi": f(inputs["w_mix_in"])[0], "wmo": f(inputs["w_mix_out"])[0],
        "w2i": f(inputs["w_ffn2_in"])[0], "w2o": f(inputs["w_ffn2_out"])[0],
        "gains": gains, "convp": convp, "rowp": rowp, "masks": np.ascontiguousarray(masks), "identb": identb,
    }
    in_maps = []
    for c in range(NCORES):
        bidx, half = c // 2, c % 2
        def lay(hh):
            xs = x[bidx, hh * NT:(hh + 1) * NT, :]
            return np.ascontiguousarray(xs.T.reshape(8, 128, NT).transpose(1, 0, 2))
        sel = np.zeros((128, 16), np.float32)
        sel[:, 0] = 1.0 if half == 1 else 0.0
        sel[:, 1] = 1.0 if half == 0 else 0.0
        m = dict(common)
        m["xT"] = lay(half)
        m["xP"] = lay(1 - half)
        m["sel"] = sel
        in_maps.append(m)
    return in_maps


def run(inputs, stage=STAGE_FULL):
    if stage not in _CACHE:
        _CACHE[stage] = build(stage)
    nc = _CACHE[stage]
    in_maps = _prep(inputs, stage)
    res = run_bass_kernel_spmd(nc, in_maps, core_ids=list(range(NCORES)))
    out = np.empty((4, 4096, D), np.float32)
    for c in range(NCORES):
        yt = np.asarray(res.results[c]["yT"])
        out[c // 2, (c % 2) * NT:(c % 2 + 1) * NT, :] = yt.transpose(1, 0, 2).reshape(D, NT).T
    return out


def kernel(**inputs):
    return run(inputs, STAGE_FULL)
```

```python
import os
import numpy as np
import ml_dtypes
from contextlib import ExitStack
import concourse.bass as bass
import concourse.mybir as mybir
from concourse.bass_utils import run_bass_kernel_spmd

F32 = mybir.dt.float32
BF16 = mybir.dt.bfloat16
AF = mybir.ActivationFunctionType
ALU = mybir.AluOpType
AX = mybir.AxisListType

D = 1024
DFF = 2816
NT = 2048
NCORES = 8
EPS = 1e-6
BIG = 60.0
STAGE_FULL = 9


class Op:
    __slots__ = ("eng", "fn", "deps", "slot", "sig", "need", "idx")


class Prog:
    def __init__(self):
        self.ops = []
        self.lw = {}
        self.rd = {}

    def op(self, eng, fn, r=(), w=(), slot=None):
        o = Op()
        o.eng, o.fn, o.slot, o.sig, o.need, o.idx = eng, fn, slot, None, False, len(self.ops)
        deps = {}
        for k in r:
            d = self.lw.get(k)
            if d is not None:
                deps[d.idx] = d
        for k in w:
            d = self.lw.get(k)
            if d is not None:
                deps[d.idx] = d
            for d in self.rd.get(k, ()):
                deps[d.idx] = d
        for k in r:
            self.rd.setdefault(k, []).append(o)
        for k in w:
            self.lw[k] = o
            self.rd[k] = []
        o.deps = []
        for d in deps.values():
            if d.slot is None and d.eng == eng and eng == "pe":
                continue
            d.need = True
            o.deps.append(d)
        self.ops.append(o)
        return o

    def emit(self, nc, es):
        engs = ["pe", "act", "dve", "pool", "sp"]
        sems = {e: es.enter_context(nc.semaphore("s_" + e)) for e in engs}
        slots = {}
        cnt = {e: 0 for e in engs}
        scnt = {}
        for o in self.ops:
            if o.slot is not None:
                if o.slot not in slots:
                    slots[o.slot] = es.enter_context(nc.semaphore("d_" + o.slot))
                    scnt[o.slot] = 0
                scnt[o.slot] += 16
                o.sig = (slots[o.slot], scnt[o.slot], 16)
            elif o.need:
                cnt[o.eng] += 1
                o.sig = (sems[o.eng], cnt[o.eng], 1)
        streams = {e: [o for o in self.ops if o.eng == e] for e in engs}

        def run(ename, e):
            waited = {}
            for o in streams[ename]:
                for d in o.deps:
                    s, v, _ = d.sig
                    if waited.get(s.num, 0) < v:
                        e.wait_ge(s, v)
                        waited[s.num] = v
                ins = o.fn(e)
                if o.sig is not None and ins is not None:
                    ins.then_inc(o.sig[0], o.sig[2])

        block = es.enter_context(nc.Block())
        block.tensor(lambda e: run("pe", e))
        block.scalar(lambda e: run("act", e))
        block.vector(lambda e: run("dve", e))
        block.gpsimd(lambda e: run("pool", e))
        block.sync(lambda e: run("sp", e))


def bc(ap, pattern):
    return bass.AP(ap.tensor, ap.offset, [list(ap.ap[0])] + [list(x) for x in pattern])


def build(stage=STAGE_FULL):
    nc = bass.Bass("TRN2", target_bir_lowering=False)
    es = ExitStack()
    P = Prog()

    def din(name, shape, dt=F32):
        return nc.dram_tensor(name, shape, dt, kind="ExternalInput").ap()

    xT = din("xT", [128, 8, NT])
    TINY = bool(os.environ.get("KTINY"))
    if TINY:
        w1i = din("w1i", [D, 512]); w1o = din("w1o", [DFF, 128])
    else:
        w1i = din("w1i", [D, 2 * DFF]); w1o = din("w1o", [DFF, D])
    wmi = din("wmi", [D, 3088]); wmo = din("wmo", [D, D])
    w2i = din("w2i", [D, 2 * DFF]); w2o = din("w2o", [DFF, D])
    gains_d = din("gains", [128, 6, 8])
    convp_d = din("convp", [128, 4, 4])
    rowp_d = din("rowp", [128, 528])
    masks_d = din("masks", [128, 5, 128])
    identb_d = din("identb", [128, 128], BF16)
    sel_d = din("sel", [128, 16])
    yT = nc.dram_tensor("yT", [128, 8, NT], F32, kind="ExternalOutput").ap()
    FUSED = stage >= 2
    if FUSED:
        xP = din("xP", [128, 8, NT])
        pay_d = nc.dram_tensor("pay", [128, 524], F32).ap()

    def sb(name, shape, dt):
        return es.enter_context(nc.sbuf_tensor(name, shape, dt))

    X = sb("X", [128, 8, NT], F32)
    R1 = sb("R1", [128, 8192], F32)
    Hreg = sb("H", [128, 24832], BF16)
    Wreg = sb("W", [128, 16384], BF16)
    Mreg = sb("M", [128, 3072], F32)
    Greg = sb("G", [128, 1880], F32)
    STG = sb("STG", [128, 2, 512], F32)
    GAINS = sb("gains_s", [128, 6, 8], F32)
    GH = sb("gh_s", [128, 6, 8], F32)
    CONVP = sb("convp_s", [128, 4, 4], F32)
    ROWP = sb("rowp_s", [128, 528], F32)
    MASKS = sb("masks_s", [128, 5, 128], F32)
    IDENTB = sb("identb_s", [128, 128], BF16)
    SEL = sb("sel_s", [128, 16], F32)
    ONESB = sb("onesb", [128, 128], BF16)
    ONESF = sb("onesf", [128, 128], F32)
    CST = sb("cst", [128, 4], F32)
    PS = es.enter_context(nc.psum_tensor("PS", [128, 8, 512], F32))

    R1b = R1[:].bitcast(BF16)
    XN = R1b[:, 0:8192].rearrange("p (k t) -> p k t", k=8)
    HO = R1[:].rearrange("p (m t) -> p m t", m=8)
    Hh = Hreg[:, 0:22528].rearrange("p (f t) -> p f t", f=22)
    QKT = Hreg[:, 0:8192].rearrange("p (m t) -> p m t", m=4)
    VAUG = Hreg[:, 8192:16448].rearrange("p (c h v) -> p c h v", c=16, h=4)
    SGO = Hreg[:, 16448:24640].rearrange("p (c f) -> p c f", c=16)
    Hf = Hreg[:].bitcast(F32)
    BG = Hf[:, 4096:6144].rearrange("p (c t) -> p c t", c=4)
    U = Hf[:, 6144:8200].rearrange("p (c t) -> p c t", c=4)
    YC = Hreg[:, 16400:18448].rearrange("p (c t) -> p c t", c=4)
    CTMP = Hf[:, 9300:9812]
    Wb = Wreg[:]
    Wf = Wreg[:].bitcast(F32)
    SQ = [Mreg[:, 0:512].bitcast(BF16), Mreg[:, 512:1024].bitcast(BF16)]
    STD = Mreg[:, 1024:2048]
    SIL = [Mreg[:, 2048:2560], Mreg[:, 2560:3072]]

    def G3(off, a, b):
        return Greg[:, off:off + a * b].rearrange("p (a b) -> p a b", a=a)
    GR = G3(0, 16, 16)
    LI = G3(256, 16, 8); LF = G3(384, 16, 8); ZZ = G3(512, 16, 8); AZ = G3(640, 16, 8)
    BC = G3(768, 16, 8); TOT = G3(896, 16, 8); EW = G3(1024, 16, 8); EB = G3(1152, 16, 8)
    EGt = G3(1280, 16, 8); GAF = G3(1408, 16, 8); EWA = G3(1536, 16, 8); LIB = G3(1664, 16, 8)
    UB = G3(1792, 4, 8)
    HALO = G3(1824, 4, 2)
    UBT = Greg[:, 1832:1864]
    DEN = Greg[:, 1864:1868]; RD = Greg[:, 1868:1872]; SSH = Greg[:, 1872:1876]


    EPS_AP = CST[:, 0:1]; NBIG_AP = CST[:, 1:2]; ONE_AP = CST[:, 2:3]

    def kH(lo, hi):
        return [("H", g) for g in range(lo // 128, (hi - 1) // 128 + 1)]

    def kW(lo, hi):
        return [("W", g) for g in range(lo // 256, (hi - 1) // 256 + 1)]

    def kR(lo, hi):
        return [("R1", g) for g in range(lo // 512, (hi - 1) // 512 + 1)]

    def kXN(k, n=1024):
        return kR(k * 512, k * 512 + n // 2)

    def kHO(m):
        return kR(m * 1024, (m + 1) * 1024)

    def kHh(f, hf):
        return kH(f * 1024 + hf * 512, f * 1024 + hf * 512 + 512)

    def kQK(m, c0, c1):
        return kH(m * 2048 + c0, m * 2048 + c1)

    def kQKall(c0, c1):
        return [k for m in range(4) for k in kQK(m, c0, c1)]

    def kVA(c):
        return kH(8192 + c * 516, 8192 + (c + 1) * 516)

    def kSGO(c):
        return kH(16448 + c * 512, 16448 + (c + 1) * 512)

    def kBG(mt):
        return kH(8192 + mt * 1024, 8192 + (mt + 1) * 1024)

    def kU(mt):
        return kH(12288 + mt * 1028, 12288 + (mt + 1) * 1028)

    def kYC(ct):
        return kH(16400 + ct * 512, 16400 + (ct + 1) * 512)

    kCTMP = kH(18600, 19624)

    def dma(eng, out, in_, r, w, slot):
        return P.op(eng, lambda e: e.dma_start(out=out, in_=in_), r=r, w=w, slot=slot)

    def xk(hb, k):
        return ("x", hb, k)

    STGS = [(STG[:, 0, :], [("stg", 0)]), (STG[:, 1, :], [("stg", 1)]),
            (Hf[:, 11264:11776], [("stg", 2)] + kH(22528, 23552)), (Hf[:, 11776:12288], [("stg", 3)] + kH(23552, 24576))]
    wq = {"pend": [], "infl": [], "n": 0, "nslots": 2, "ce": 0}

    def set_slots(n):
        assert not wq["pend"] and not wq["infl"]
        wq["nslots"] = n
        wq["n"] = 0

    def pump():
        while wq["pend"] and len(wq["infl"]) < wq["nslots"]:
            dst_p, src_p, keys, shp = wq["pend"].pop(0)
            sl = wq["n"] % wq["nslots"]
            wq["n"] += 1
            stf, sk = STGS[sl]
            st = stf[:, 0:shp[0] * shp[1]].rearrange("p (a b) -> p a b", a=shp[0])
            P.op("sp", lambda e, st=st, src_p=src_p: e.dma_start(out=st, in_=src_p), r=[], w=sk, slot="stg%d" % sl)
            wq["infl"].append((dst_p, st, sk, keys))

    def enqueue(dst, src, keys):
        A, B = dst.shape[1], dst.shape[2]
        Bc = min(B, 512)
        ra = max(1, 512 // Bc)
        for b0 in range(0, B, Bc):
            for a0 in range(0, A, ra):
                r_ = min(ra, A - a0)
                wq["pend"].append((dst[:, a0:a0 + r_, b0:b0 + Bc], src[:, a0:a0 + r_, b0:b0 + Bc], keys, (r_, Bc)))
        pump()

    def cast_some(n):
        for _ in range(n):
            if not wq["infl"]:
                pump()
            if not wq["infl"]:
                return
            dst_p, st, sk, keys = wq["infl"].pop(0)
            wq["ce"] += 1
            if wq["ce"] % 2:
                P.op("act", lambda e, dst_p=dst_p, st=st: e.activation(out=dst_p, in_=st, func=AF.Copy), r=sk, w=keys)
            else:
                P.op("dve", lambda e, dst_p=dst_p, st=st: e.tensor_copy(out=dst_p, in_=st), r=sk, w=keys)
            pump()

    def cast_all():
        while wq["pend"] or wq["infl"]:
            cast_some(1)

    def wload(dst, src, keys):
        enqueue(dst, src, keys)
        cast_all()

    def load_x(src, eng):
        for b in range(2):
            for k in range(8):
                dma(eng, X[:, k, b * 1024:(b + 1) * 1024], src[:, k, b * 1024:(b + 1) * 1024], [], [xk(2 * b, k), xk(2 * b + 1, k)],
                    "xin%d_%d" % (k, b))

    dma("sp", GAINS[:], gains_d, [], ["gains"], "c0")
    dma("sp", CONVP[:], convp_d, [], ["convp"], "c1")
    dma("sp", ROWP[:], rowp_d, [], ["rowp"], "c2")
    dma("sp", MASKS[:], masks_d, [], ["masks"], "c3")
    dma("sp", IDENTB[:], identb_d, [], ["identb"], "c4")
    dma("sp", SEL[:], sel_d, [], ["sel"], "c5")
    load_x((xP if FUSED else xT), "sp")
    P.op("dve", lambda e: e.memset(ONESB[:], 1.0), w=["onesb"])
    P.op("dve", lambda e: e.memset(ONESF[:], 1.0), w=["onesf"])
    P.op("dve", lambda e: e.memset(CST[:, 0:1], EPS), w=["cst"])
    P.op("dve", lambda e: e.memset(CST[:, 1:2], -BIG), w=["cst"])
    P.op("dve", lambda e: e.memset(CST[:, 2:3], 1.0), w=["cst"])
    for gi, sc in enumerate([1.0, 0.5, 1.0, 1.0, 1.0, 0.5]):
        P.op("dve", lambda e, gi=gi, sc=sc: e.tensor_scalar(out=GH[:, gi, :], in0=GAINS[:, gi, :], scalar1=sc,
                                                          scalar2=None, op0=ALU.mult), r=["gains"], w=["gh"])

    def prenorm(c0, n, gi, xv=None, xkeys=None):
        if xv is None:
            xv = XN[:, :, 0:n]
            xkeys = lambda k: kXN(k, n)
        nb = n // 512
        hbs = [c0 // 512 + i for i in range(nb)]
        for k in range(8):
            sq = SQ[k % 2][:, 0:n]
            P.op("act", lambda e, k=k, sq=sq: e.activation(out=sq, in_=X[:, k, c0:c0 + n], func=AF.Square),
                 r=[xk(h, k) for h in hbs], w=[("sq", k % 2)])
            for hf in range(nb):
                P.op("pe", lambda e, k=k, sq=sq, hf=hf: e.matmul(PS[:, 6 + hf, :], lhsT=ONESB[:], rhs=sq[:, hf * 512:(hf + 1) * 512],
                                                              start=(k == 0), stop=(k == 7)),
                     r=[("sq", k % 2), "onesb"], w=[("ps", 6 + hf)])
        rstd(n)
        for k in range(8):
            P.op("dve", lambda e, k=k: e.scalar_tensor_tensor(out=xv[:, k, :], in0=X[:, k, c0:c0 + n], scalar=GAINS[:, gi, k:k + 1],
                                                            in1=STD[:, 0:n], op0=ALU.mult, op1=ALU.mult),
                 r=[xk(h, k) for h in hbs] + ["std", "gains"], w=xkeys(k))

    def rstd(n):
        nb = n // 512
        P.op("act", lambda e: e.activation(out=STD[:, 0:n].rearrange("p (a b) -> p a b", a=nb), in_=PS[:, 6:6 + nb, :], func=AF.Ln,
                                          scale=1.0 / D, bias=EPS_AP),
             r=[("ps", 6 + i) for i in range(nb)] + ["cst"], w=["std"])
        P.op("act", lambda e: e.activation(out=STD[:, 0:n], in_=STD[:, 0:n], func=AF.Exp, scale=-0.5), r=["std"], w=["std"])

    wctr = {"i": 0, "o": 0}

    def ffn(b, w_in, w_out, gpre, gpost, dbg=None):
        c0 = b * 1024
        w_in_v = w_in.rearrange("(k p) f -> p k f", p=128)
        w_out_v = w_out.rearrange("(f p) m -> p f m", p=128)
        set_slots(4)
        prenorm(c0, 1024, gpre)
        pcnt = 0

        def enq_in(grp):
            bi = wctr["i"] % 2
            wctr["i"] += 1
            WGt = Wb[:, bi * 4096:bi * 4096 + 2048].rearrange("p (k f) -> p k f", k=8)
            WUt = Wb[:, bi * 4096 + 2048:bi * 4096 + 4096].rearrange("p (k f) -> p k f", k=8)
            kg = kW(bi * 4096, bi * 4096 + 2048)
            ku = kW(bi * 4096 + 2048, bi * 4096 + 4096)
            gsl = slice(0, 256) if TINY else slice(grp * 256, (grp + 1) * 256)
            usl = slice(256, 512) if TINY else slice(DFF + grp * 256, DFF + (grp + 1) * 256)
            enqueue(WGt, w_in_v[:, :, gsl], kg)
            enqueue(WUt, w_in_v[:, :, usl], ku)
            return WGt, WUt, kg, ku

        def enq_out(m):
            bo = wctr["o"] % 2
            wctr["o"] += 1
            WOt = Wb[:, 8192 + bo * 2816:8192 + (bo + 1) * 2816].rearrange("p (f m) -> p f m", f=22)
            ko = kW(8192 + bo * 2816, 8192 + (bo + 1) * 2816)
            enqueue(WOt, w_out_v[:, :, (slice(0, 128) if TINY else slice(m * 128, (m + 1) * 128))], ko)
            return WOt, ko

        ginfo = {0: enq_in(0)}
        cast_all()
        oinfo = {}
        for grp in range(11):
            if grp + 1 < 11:
                ginfo[grp + 1] = enq_in(grp + 1)
            else:
                oinfo[0] = enq_out(0)
            WGt, WUt, kg, ku = ginfo[grp]
            for hf in range(2):
                for fi in range(2):
                    f = grp * 2 + fi
                    gb = pcnt % 2
                    ub = 2 + pcnt % 2
                    s = pcnt % 2
                    pcnt += 1
                    for k in range(8):
                        P.op("pe", lambda e, k=k, gb=gb, fi=fi, hf=hf, WGt=WGt: e.matmul(
                            PS[:, gb, :], lhsT=WGt[:, k, fi * 128:(fi + 1) * 128], rhs=XN[:, k, hf * 512:(hf + 1) * 512],
                            start=(k == 0), stop=(k == 7)), r=kg + kXN(k), w=[("ps", gb)])
                    for k in range(8):
                        P.op("pe", lambda e, k=k, ub=ub, fi=fi, hf=hf, WUt=WUt: e.matmul(
                            PS[:, ub, :], lhsT=WUt[:, k, fi * 128:(fi + 1) * 128], rhs=XN[:, k, hf * 512:(hf + 1) * 512],
                            start=(k == 0), stop=(k == 7)), r=ku + kXN(k), w=[("ps", ub)])
                    P.op("act", lambda e, gb=gb, s=s: e.activation(out=SIL[s], in_=PS[:, gb, :], func=AF.Silu),
                         r=[("ps", gb)], w=[("sil", s)])
                    P.op("dve", lambda e, ub=ub, s=s, f=f, hf=hf: e.tensor_tensor(
                        out=Hh[:, f, hf * 512:(hf + 1) * 512], in0=SIL[s], in1=PS[:, ub, :], op=ALU.mult),
                        r=[("sil", s), ("ps", ub)], w=kHh(f, hf))
                    cast_some(2)
            cast_all()
        if dbg == 0.6:
            return
        for m in range(8):
            if m + 1 < 8:
                oinfo[m + 1] = enq_out(m + 1)
                cast_all()
            WOt, ko = oinfo[m]
            ob = [4, 0, 2][m % 3]
            for f in range(22):
                for hf in range(2):
                    P.op("pe", lambda e, f=f, hf=hf, ob=ob, WOt=WOt: e.matmul(
                        PS[:, ob + hf, :], lhsT=WOt[:, f, :], rhs=Hh[:, f, hf * 512:(hf + 1) * 512],
                        start=(f == 0), stop=(f == 21)), r=ko + kHh(f, hf), w=[("ps", ob + hf)])
            sq = SQ[m % 2]
            if dbg == 0.61:
                P.op("act", lambda e, ob=ob, m=m: e.activation(out=HO[:, m, :].rearrange("p (a b) -> p a b", a=2), in_=PS[:, ob:ob + 2, :],
                                                             func=AF.Copy), r=[("ps", ob), ("ps", ob + 1)], w=kHO(m))
                continue
            P.op("act", lambda e, ob=ob, sq=sq: e.activation(out=sq.rearrange("p (a b) -> p a b", a=2), in_=PS[:, ob:ob + 2, :],
                                                           func=AF.Square), r=[("ps", ob), ("ps", ob + 1)], w=[("sq", m % 2)])
            for hf in range(2):
                P.op("pe", lambda e, sq=sq, hf=hf, m=m: e.matmul(PS[:, 6 + hf, :], lhsT=ONESB[:], rhs=sq[:, hf * 512:(hf + 1) * 512],
                                                              start=(m == 0), stop=(m == 7)),
                     r=[("sq", m % 2), "onesb"], w=[("ps", 6 + hf)])
            P.op("dve", lambda e, ob=ob, m=m: e.tensor_scalar(out=HO[:, m, :].rearrange("p (a b) -> p a b", a=2), in0=PS[:, ob:ob + 2, :],
                                                            scalar1=GH[:, gpost, m:m + 1], scalar2=None, op0=ALU.mult),
                 r=[("ps", ob), ("ps", ob + 1), "gh", ("sq", m % 2)], w=kHO(m))
        if dbg in (0.61, 0.65):
            return
        rstd(1024)
        for m in range(8):
            P.op("dve", lambda e, m=m: e.tensor_tensor(out=HO[:, m, :], in0=HO[:, m, :], in1=STD[:, 0:1024], op=ALU.mult),
                 r=kHO(m) + ["std"], w=kHO(m))
            P.op("dve", lambda e, m=m: e.tensor_tensor(out=X[:, m, c0:c0 + 1024], in0=X[:, m, c0:c0 + 1024], in1=HO[:, m, :], op=ALU.add),
                 r=kHO(m) + [xk(2 * b, m), xk(2 * b + 1, m)], w=[xk(2 * b, m), xk(2 * b + 1, m)])
        set_slots(2)

    def store_out(b):
        for k in range(8):
            dma("sp", yT[:, k, b * 1024:(b + 1) * 1024], X[:, k, b * 1024:(b + 1) * 1024], [xk(2 * b, k), xk(2 * b + 1, k)],
                [("y", b, k)], "yo%d_%d" % (b, k))

    def finish():
        keys = list(P.lw.keys())
        P.op("sp", lambda e: None, r=[k for k in keys if isinstance(k, tuple) and k[0] == "y"])
        P.emit(nc, es)
        es.close()
        return nc

    if stage == 0:
        for b in range(2):
            store_out(b)
        return finish()
    if stage == 0.5:
        prenorm(0, 1024, 0)
        for b in range(2):
            store_out(b)
        return finish()
    for b in range(2):
        ffn(b, w1i, w1o, 0, 1, stage)
        if stage in (0.6, 0.61, 0.65, 0.7):
            break
    if stage <= 1:
        for b in range(2):
            store_out(b)
        return finish()

    wmi_v = wmi.rearrange("(k p) f -> p k f", p=128)

    def wtile(lo, n, kdim=8):
        return Wb[:, lo:lo + n].rearrange("p (k f) -> p k f", k=kdim), kW(lo, lo + n)

    WQK, kWQK = wtile(0, 4096)
    WV, kWV = wtile(4096, 4096)
    WOG, kWOG = wtile(8192, 4096)
    WGA, kWGA = wtile(12288, 128)
    WBt = [wtile(12544 + i * 1024, 1024) for i in range(2)]
    XB, kXB = wtile(14592, 32)
    GROW = ROWP[:, 0:512]
    BI = bc(ROWP[:, 512:520], [[0, 16], [1, 8]])
    BF_ = bc(ROWP[:, 520:528], [[0, 16], [1, 8]])
    gk = ["gates"]
    RQf = Wb[:, 13568:14080]; kRQ = kW(13568, 14080)
    RQ = RQf.rearrange("p (a b) -> p a b", a=2)
    SBFf = Wb[:, 14080:15112]; kSBF = kW(14080, 15112)
    SBF = SBFf.rearrange("p (a b) -> p a b", a=4)

    def ksbf(i):
        return kSBF

    def ftile(lo, n):
        return Wf[:, lo:lo + n], kW(2 * lo, 2 * (lo + n))

    def btile(lo, n):
        return Wb[:, lo:lo + n], kW(lo, lo + n)

    E_t, kE = ftile(0, 512)
    L1f, kL1 = ftile(512, 512); L1 = L1f.rearrange("p (h s) -> p h s", h=4)
    L2f, kL2 = ftile(1024, 512); L2 = L2f.rearrange("p (h s) -> p h s", h=4)
    TIf, kTI = ftile(1536, 516); TI = TIf.rearrange("p (h v) -> p h v", h=4)
    NDf, kND = ftile(2052, 516); ND = NDf.rearrange("p (h v) -> p h v", h=4)
    HHf, kHH = ftile(2568, 512); HHt = HHf.rearrange("p (h v) -> p h v", h=4)
    SQT, kSQT = ftile(3080, 512)
    PT, kPT = btile(7184, 512)
    KT2f, kKT2 = btile(7696, 256); KT2 = KT2f.rearrange("p (h k) -> p h k", h=4)
    YTOK, kYTOK = btile(8208, 512)
    PT_b, kPT_b = btile(15360, 512)
    KT2f_b, kKT2_b = btile(15872, 256); KT2_b = KT2f_b.rearrange("p (h k) -> p h k", h=4)
    PTs = [(PT, kPT), (PT_b, kPT_b)]
    KT2s = [(KT2, kKT2), (KT2_b, kKT2_b)]
    KTAs = [btile(9232 + i * 512, 512) for i in range(2)]
    EXS, kEXS = ftile(5128, 524)
    SSTs = [[ftile(4736 + (d * 2 + p) * 256, 129) for p in range(2)] for d in range(2)]
    ACCS, kACCS = ftile(6200, 524)
    EXG = R1[:, 0:4192].rearrange("p (r c) -> p r c", r=8)
    kEXG = kR(0, 4192)

    def ktok_transposes(c, bank):
        cs = slice(c * 128, (c + 1) * 128)
        kb = PS[:, bank, 0:128].bitcast(BF16)
        for p in range(2):
            P.op("pe", lambda e, p=p, kb=kb, cs=cs: e.transpose(out=kb[:, p * 128:(p + 1) * 128], in_=QKT[:, 2 + p, cs], identity=IDENTB[:]),
                 r=kQK(2 + p, c * 128, (c + 1) * 128) + ["identb"], w=[("ps", bank)])
        return kb


    def mixer_front(do_pass1):
        set_slots(4 if do_pass1 else 2)
        enqueue(WQK, wmi_v[:, :, 1536:2048], kWQK)
        enqueue(WV, wmi_v[:, :, 2048:2560], kWV)
        if not do_pass1:
            enqueue(WOG, wmi_v[:, :, 2560:3072], kWOG)
        enqueue(WGA, wmi_v[:, :, 3072:3088], kWGA)
        P.op("dve", lambda e: e.memset(VAUG[:, :, :, 128:129], 1.0), w=[k for c in range(16) for k in kVA(c)])
        cnt2 = 0
        wbc = 0
        def enq_wb(m8):
            nonlocal wbc
            WBw, kWBw = WBt[wbc % 2]
            wbc += 1
            enqueue(WBw, wmi_v[:, :, 512 + m8 * 128:512 + (m8 + 1) * 128], kWBw)
            return WBw, kWBw

        XNs = [XN, R1b[:, 8192:16384].rearrange("p (k t) -> p k t", k=8)]
        kXNs = [lambda k: kXN(k), lambda k: kR(4096 + k * 512, 4096 + (k + 1) * 512)]
        prenorm(0, 1024, 2, XNs[0], kXNs[0])
        cast_all()
        prenorm(1024, 1024, 2, XNs[1], kXNs[1])
        if do_pass1:
            load_x(xT, "pool")
        for b in range(2):
            XNb, kXNb = XNs[b], kXNs[b]
            wbi = {0: enq_wb(0)}
            for m in range(2 if do_pass1 else 0, 4):
                for hf in range(2):
                    pb = cnt2 % 2
                    cnt2 += 1
                    for k in range(8):
                        P.op("pe", lambda e, XNb=XNb, k=k, m=m, hf=hf, pb=pb: e.matmul(PS[:, pb, :], lhsT=WQK[:, k, m * 128:(m + 1) * 128],
                                                                           rhs=XNb[:, k, hf * 512:(hf + 1) * 512], start=(k == 0), stop=(k == 7)),
                             r=kWQK + kXNb(k), w=[("ps", pb)])
                    cs = b * 1024 + hf * 512
                    P.op("act", lambda e, m=m, pb=pb, cs=cs: e.activation(out=QKT[:, m, cs:cs + 512], in_=PS[:, pb, :], func=AF.Copy,
                                                                         scale=(0.125 if m < 2 else 1.0)),
                         r=[("ps", pb)], w=kQK(m, cs, cs + 512))
            for cc in range(8):
                c = b * 8 + cc
                tc_ = slice(cc * 128, (cc + 1) * 128)
                vb = 2 + cc % 2
                obk = 4 + cc % 2
                for k in range(8):
                    P.op("pe", lambda e, XNb=XNb, k=k, tc_=tc_, vb=vb: e.matmul(PS[:, vb, :], lhsT=XNb[:, k, tc_], rhs=WV[:, k, :], start=(k == 0), stop=(k == 7)),
                         r=kWV + kXNb(k), w=[("ps", vb)])
                P.op("dve", lambda e, c=c, vb=vb: e.tensor_copy(out=VAUG[:, c, :, 0:128], in_=PS[:, vb, :].rearrange("p (h v) -> p h v", h=4)),
                     r=[("ps", vb)], w=kVA(c))
                if not do_pass1:
                    for k in range(8):
                        P.op("pe", lambda e, XNb=XNb, k=k, tc_=tc_, obk=obk: e.matmul(PS[:, obk, :], lhsT=XNb[:, k, tc_], rhs=WOG[:, k, :], start=(k == 0), stop=(k == 7)),
                             r=kWOG + kXNb(k), w=[("ps", obk)])
                    P.op("act", lambda e, obk=obk: e.activation(out=SIL[0], in_=PS[:, obk, :], func=AF.Sigmoid), r=[("ps", obk)], w=[("sil", 0)])
                    P.op("dve", lambda e, c=c: e.tensor_tensor(out=SGO[:, c, :], in0=SIL[0], in1=GROW, op=ALU.mult),
                         r=[("sil", 0), "rowp"], w=kSGO(c))
                for k in range(8):
                    P.op("pe", lambda e, XNb=XNb, k=k, tc_=tc_, cc=cc: e.matmul(PS[:, 7, cc * 16:(cc + 1) * 16], lhsT=XNb[:, k, tc_], rhs=WGA[:, k, :],
                                                                    start=(k == 0), stop=(k == 7)),
                         r=kWGA + kXNb(k), w=[("ps", 7)])
            P.op("dve", lambda e, XNb=XNb, b=b: e.tensor_copy(out=GR[:, b * 8:(b + 1) * 8, :], in_=PS[:, 7, 0:128].rearrange("p (c g) -> p c g", c=8)),
                 r=[("ps", 7)], w=["gr"])
            allxn = [k_ for k in range(8) for k_ in kXNb(k)]
            P.op("dve", lambda e, XNb=XNb: e.tensor_copy(out=XB[:, :, 0:2], in_=bc(XNb[:, :, 0:1], [[1024, 8], [512, 2]])), r=allxn, w=kXB)
            P.op("dve", lambda e, XNb=XNb: e.tensor_copy(out=XB[:, :, 2:4], in_=bc(XNb[:, :, 511:512], [[1024, 8], [512, 2]])), r=allxn, w=kXB)
            for m8 in range(8):
                cast_all()
                if m8 + 1 < 8:
                    wbi[m8 + 1] = enq_wb(m8 + 1)
                WBw, kWBw = wbi[m8]
                for k in range(8):
                    P.op("pe", lambda e, k=k, m8=m8, WBw=WBw: e.matmul(PS[:, 6, m8 * 4:(m8 + 1) * 4], lhsT=WBw[:, k, :], rhs=XB[:, k, :],
                                                                    start=(k == 0), stop=(k == 7)),
                         r=kWBw + kXB, w=[("ps", 6)])
            P.op("act", lambda e: e.activation(out=UBT[:, 0:16], in_=PS[:, 6, 0:16], func=AF.Copy), r=[("ps", 6)], w=["ubt"])
            P.op("dve", lambda e, b=b: e.tensor_tensor(out=UB[:, :, b * 4:(b + 1) * 4], in0=UBT[:, 0:16].rearrange("p (c t) -> p c t", c=4),
                                                     in1=PS[:, 6, 16:32].rearrange("p (c t) -> p c t", c=4), op=ALU.mult),
                 r=["ubt", ("ps", 6)], w=["ub"])

        P.op("dve", lambda e: e.tensor_tensor(out=LI, in0=GR[:, :, 0:8], in1=BI, op=ALU.add), r=["gr", "rowp"], w=gk)
        P.op("dve", lambda e: e.tensor_tensor(out=ZZ, in0=GR[:, :, 8:16], in1=BF_, op=ALU.add), r=["gr", "rowp"], w=gk)
        P.op("act", lambda e: e.activation(out=AZ, in_=ZZ, func=AF.Abs), r=gk, w=gk)
        P.op("act", lambda e: e.activation(out=AZ, in_=AZ, func=AF.Exp, scale=-1.0), r=gk, w=gk)
        P.op("act", lambda e: e.activation(out=AZ, in_=AZ, func=AF.Ln, bias=ONE_AP), r=gk + ["cst"], w=gk)
        P.op("dve", lambda e: e.tensor_scalar(out=ZZ, in0=ZZ, scalar1=0.0, scalar2=None, op0=ALU.min), r=gk, w=gk)
        P.op("dve", lambda e: e.tensor_tensor(out=LF, in0=ZZ, in1=AZ, op=ALU.subtract), r=gk, w=gk)
        P.op("dve", lambda e: e.tensor_scalar(out=LIB, in0=LI, scalar1=BIG, scalar2=None, op0=ALU.add), r=gk, w=gk)
        P.op("pe", lambda e: e.matmul(PS[:, 0, 0:64], lhsT=MASKS[:, 0, :], rhs=LF[:, :, 0:4], start=True, stop=True),
             r=gk + ["masks"], w=[("ps", 0)])
        P.op("pe", lambda e: e.matmul(PS[:, 0, 64:128], lhsT=MASKS[:, 1, :], rhs=LF[:, :, 4:8], start=True, stop=True),
             r=gk + ["masks"], w=[("ps", 0)])
        P.op("pe", lambda e: e.matmul(PS[:, 0, 128:256], lhsT=ONESF[:], rhs=LF.rearrange("p c j -> p (c j)"), start=True, stop=True),
             r=gk + ["onesf"], w=[("ps", 0)])
        P.op("dve", lambda e: e.tensor_copy(out=BC[:, :, 0:4], in_=PS[:, 0, 0:64].rearrange("p (c j) -> p c j", c=16)), r=[("ps", 0)], w=gk)
        P.op("dve", lambda e: e.tensor_copy(out=BC[:, :, 4:8], in_=PS[:, 0, 64:128].rearrange("p (c j) -> p c j", c=16)), r=[("ps", 0)], w=gk)
        P.op("dve", lambda e: e.tensor_copy(out=TOT, in_=PS[:, 0, 128:256].rearrange("p (c j) -> p c j", c=16)), r=[("ps", 0)], w=gk)
        P.op("dve", lambda e: e.tensor_tensor(out=EW, in0=TOT, in1=BC, op=ALU.subtract), r=gk, w=gk)
        P.op("dve", lambda e: e.tensor_tensor(out=EW, in0=EW, in1=LI, op=ALU.add), r=gk, w=gk)
        P.op("dve", lambda e: e.memset(GAF, 0.0), r=gk, w=gk)
        for c in range(14, -1, -1):
            P.op("dve", lambda e, c=c: e.tensor_tensor(out=GAF[:, c, 0:4], in0=GAF[:, c + 1, 0:4], in1=TOT[:, c + 1, 0:4], op=ALU.add), r=gk, w=gk)
        for c in range(1, 16):
            P.op("dve", lambda e, c=c: e.tensor_tensor(out=GAF[:, c, 4:8], in0=GAF[:, c - 1, 4:8], in1=TOT[:, c - 1, 4:8], op=ALU.add), r=gk, w=gk)
        P.op("dve", lambda e: e.tensor_tensor(out=EWA, in0=EW, in1=GAF, op=ALU.add), r=gk, w=gk)
        P.op("act", lambda e: e.activation(out=EWA, in_=EWA, func=AF.Exp), r=gk, w=gk)
        P.op("act", lambda e: e.activation(out=EW, in_=EW, func=AF.Exp), r=gk, w=gk)
        P.op("act", lambda e: e.activation(out=EB, in_=BC, func=AF.Exp), r=gk, w=gk)
        P.op("act", lambda e: e.activation(out=EGt, in_=TOT, func=AF.Exp), r=gk, w=gk)

        if do_pass1:
            for c in range(16):
                bank = 4 + c % 2
                kb = ktok_transposes(c, bank)
                KTAf, kKTA = KTAs[c % 2]
                KTA = KTAf.rearrange("p (d h k) -> p d h k", d=2, h=4)
                P.op("dve", lambda e, KTA=KTA, kb=kb, c=c: e.tensor_tensor(
                    out=KTA, in0=bc(kb, [[0, 2], [64, 4], [1, 64]]), in1=bc(EWA[:, c, :], [[4, 2], [1, 4], [0, 64]]), op=ALU.mult),
                    r=[("ps", bank)] + gk, w=kKTA)
                for d in range(2):
                    for p in range(2):
                        P.op("pe", lambda e, KTA=KTA, d=d, p=p, c=c: e.matmul(
                            PS[:, d * 2 + p, 0:258], lhsT=KTA[:, d, 2 * p:2 * p + 2, :].rearrange("p h k -> p (h k)"),
                            rhs=VAUG[:, c, 2 * p:2 * p + 2, :].rearrange("p h v -> p (h v)"), start=(c == 0), stop=(c == 15)),
                            r=kKTA + kVA(c), w=[("ps", d * 2 + p)])
            for d in range(2):
                for p in range(2):
                    i = d * 2 + p
                    P.op("dve", lambda e, i=i: e.tensor_copy(out=EXS[0:64, i * 129:(i + 1) * 129], in_=PS[0:64, i, 0:129]), r=[("ps", i)], w=kEXS)
                    P.op("dve", lambda e, i=i: e.tensor_copy(out=EXS[64:128, i * 129:(i + 1) * 129], in_=PS[64:128, i, 129:258]), r=[("ps", i)], w=kEXS)
            P.op("dve", lambda e: e.tensor_copy(out=EXS[:, 516:520], in_=UB[:, :, 0]), r=["ub"], w=kEXS)
            P.op("dve", lambda e: e.tensor_copy(out=EXS[:, 520:524], in_=UB[:, :, 7]), r=["ub"], w=kEXS)

    mixer_front(True)
    dma("sp", pay_d, EXS, kEXS, ["pay"], "payo")
    for b in range(2):
        ffn(b, w1i, w1o, 0, 1)
    mixer_front(False)
    P.op("dve", lambda e: e.memset(SBFf, 0.0), w=kSBF)
    P.op("dve", lambda e: e.memset(RQf, 0.0), w=kRQ)
    dma("sp", ACCS, pay_d, ["pay"], kACCS, "payi")
    for (lo, hi, so) in [(0, 258, 0), (520, 524, 0), (258, 520, 1)]:
        P.op("dve", lambda e, lo=lo, hi=hi, so=so: e.tensor_scalar(out=ACCS[:, lo:hi], in0=ACCS[:, lo:hi], scalar1=SEL[:, so:so + 1],
                                                              scalar2=None, op0=ALU.mult), r=kACCS + ["sel"], w=kACCS)
    for d in range(2):
        for p in range(2):
            i = d * 2 + p
            SSt, kSS = SSTs[d][p]
            P.op("dve", lambda e, SSt=SSt, i=i: e.tensor_copy(out=SSt, in_=ACCS[:, i * 129:(i + 1) * 129]), r=kACCS, w=kSS)
            P.op("act", lambda e, i=i: e.activation(out=SBF[0:64, i, 0:129], in_=ACCS[0:64, i * 129:(i + 1) * 129], func=AF.Copy),
                 r=kACCS, w=ksbf(i))
            P.op("act", lambda e, i=i: e.activation(out=SBF[64:128, i, 129:258], in_=ACCS[64:128, i * 129:(i + 1) * 129], func=AF.Copy),
                 r=kACCS, w=ksbf(i))
    P.op("dve", lambda e: e.tensor_copy(out=HALO[:, :, 0], in_=ACCS[:, 520:524]), r=kACCS, w=["halo"])
    P.op("dve", lambda e: e.tensor_copy(out=HALO[:, :, 1], in_=ACCS[:, 516:520]), r=kACCS, w=["halo"])

    HS = R1[:].rearrange("p (c f) -> p c f", c=16)
    arrived = [0] * 16

    def kHS(c):
        return kR(c * 512, (c + 1) * 512)

    def fin_a(c):
        hsf = HS[:, c, :]
        P.op("dve", lambda e: e.tensor_tensor(out=SQT, in0=hsf, in1=hsf, op=ALU.mult), r=kHS(c), w=kSQT)
        P.op("dve", lambda e: e.tensor_reduce(out=SSH, in_=SQT.rearrange("p (h v) -> p h v", h=4), axis=AX.X, op=ALU.add), r=kSQT, w=["ssh"])
        P.op("act", lambda e: e.activation(out=SSH, in_=SSH, func=AF.Ln, scale=1.0 / 128, bias=EPS_AP), r=["ssh", "cst"], w=["ssh"])
        P.op("act", lambda e: e.activation(out=SSH, in_=SSH, func=AF.Exp, scale=-0.5), r=["ssh"], w=["ssh"])
        P.op("dve", lambda e: e.tensor_tensor(out=SQT.rearrange("p (h v) -> p h v", h=4), in0=hsf.rearrange("p (h v) -> p h v", h=4),
                                             in1=bc(SSH, [[1, 4], [0, 128]]), op=ALU.mult), r=kHS(c) + ["ssh"], w=kSQT)
        P.op("dve", lambda e: e.tensor_tensor(out=YTOK, in0=SQT, in1=SGO[:, c, :], op=ALU.mult), r=kSQT + kSGO(c), w=kYTOK)

    def fin_b(c):
        cs = slice(c * 128, (c + 1) * 128)
        yb = PS[:, 1, 0:256].bitcast(BF16)
        for ft in range(4):
            P.op("pe", lambda e, ft=ft: e.transpose(out=yb[:, ft * 128:(ft + 1) * 128], in_=YTOK[:, ft * 128:(ft + 1) * 128], identity=IDENTB[:]),
                 r=kYTOK + ["identb"], w=[("ps", 1)])
        P.op("act", lambda e: e.activation(out=QKT[:, :, cs], in_=yb.rearrange("p (f t) -> p f t", f=4), func=AF.Copy),
             r=[("ps", 1)], w=kQKall(c * 128, (c + 1) * 128))

    pending_fin = []

    def stepA(c, d, par):
        cs = slice(c * 128, (c + 1) * 128)
        c0_, c1_ = c * 128, (c + 1) * 128
        j0 = d * 4
        PT, kPT = PTs[par]
        KT2, kKT2 = KT2s[par]
        msu = MASKS[:, 2 + d, :]
        P.op("dve", lambda e: e.tensor_tensor(out=L1, in0=bc(msu, [[0, 4], [1, 128]]), in1=bc(LF[:, c, j0:j0 + 4], [[1, 4], [0, 128]]), op=ALU.mult),
             r=gk + ["masks"], w=kL1)
        P.op("dve", lambda e: e.tensor_tensor(out=L2, in0=bc(MASKS[:, 4, :], [[0, 4], [1, 128]]), in1=bc(LIB[:, c, j0:j0 + 4], [[1, 4], [0, 128]]), op=ALU.mult),
             r=gk + ["masks"], w=kL2)
        P.op("dve", lambda e: e.tensor_tensor(out=L1, in0=L1, in1=L2, op=ALU.add), r=kL1 + kL2, w=kL1)
        yield
        for p in range(2):
            P.op("act", lambda e, p=p: e.activation(out=RQ[0:64, p, 0:128], in_=QKT[0:64, p, cs], func=AF.Copy),
                 r=kQK(p, c0_, c1_), w=kRQ)
            P.op("act", lambda e, p=p: e.activation(out=RQ[64:128, p, 128:256], in_=QKT[64:128, p, cs], func=AF.Copy),
                 r=kQK(p, c0_, c1_), w=kRQ)
        yield
        kb = ktok_transposes(c, 5)
        for p in range(2):
            P.op("pe", lambda e, p=p: e.matmul(PS[:, 0, p * 256:(p + 1) * 256], lhsT=QKT[:, 2 + p, cs], rhs=RQ[:, p, :], start=True, stop=True),
                 r=kQK(2 + p, c0_, c1_) + kRQ, w=[("ps", 0)])
        for h in range(4):
            P.op("pe", lambda e, h=h: e.matmul(PS[:, 1, h * 128:(h + 1) * 128], lhsT=L1[:, h, :], rhs=MASKS[:, d, :], start=True, stop=True),
                 r=kL1 + ["masks"], w=[("ps", 1)])
        yield
        P.op("act", lambda e: e.activation(out=E_t, in_=PS[:, 1, :], func=AF.Exp, bias=NBIG_AP), r=[("ps", 1), "cst"], w=kE)
        yield
        P.op("dve", lambda e: e.tensor_tensor(out=PT, in0=E_t, in1=PS[:, 0, :], op=ALU.mult), r=kE + [("ps", 0)], w=kPT)
        P.op("dve", lambda e: e.tensor_tensor(out=KT2, in0=bc(kb, [[64, 4], [1, 64]]), in1=bc(EW[:, c, j0:j0 + 4], [[1, 4], [0, 64]]), op=ALU.mult),
             r=[("ps", 5)] + gk, w=kKT2)
        yield

    def stepB(c, d, par):
        cs = slice(c * 128, (c + 1) * 128)
        c0_, c1_ = c * 128, (c + 1) * 128
        j0 = d * 4
        PT, kPT = PTs[par]
        KT2, kKT2 = KT2s[par]
        for h in range(4):
            outp = PS[:, 4, h * 129:(h + 1) * 129] if h < 3 else PS[:, 7, 258:387]
            P.op("pe", lambda e, h=h, outp=outp: e.matmul(outp, lhsT=PT[:, h * 128:(h + 1) * 128], rhs=VAUG[:, c, h, :], start=True, stop=True),
                 r=kPT + kVA(c), w=[("ps", 4 if h < 3 else 7)])
        regs = [PS[:, 2, 0:258], PS[:, 3, 0:258]]
        for p in range(2):
            P.op("pe", lambda e, p=p: e.matmul(regs[p], lhsT=KT2[:, 2 * p:2 * p + 2, :].rearrange("p h k -> p (h k)"),
                                             rhs=VAUG[:, c, 2 * p:2 * p + 2, :].rearrange("p h v -> p (h v)"), start=True, stop=True),
                 r=kKT2 + kVA(c), w=[("ps", 2 + p)])
        for p in range(2):
            P.op("pe", lambda e, p=p: e.matmul(PS[:, 6 + p, 0:258], lhsT=QKT[:, p, cs], rhs=SBF[:, d * 2 + p, :], start=True, stop=True),
                 r=kQK(p, c0_, c1_) + ksbf(d * 2 + p), w=[("ps", 6 + p)])
        yield
        for h in range(4):
            inp = PS[:, 6 + h // 2, (h % 2) * 129:(h % 2 + 1) * 129]
            P.op("act", lambda e, h=h, inp=inp: e.activation(out=TI[:, h, :], in_=inp, func=AF.Copy, scale=EB[:, c, j0 + h:j0 + h + 1]),
                 r=[("ps", 6 + h // 2)] + gk, w=kTI)
        yield
        P.op("dve", lambda e: e.tensor_tensor(out=ND[:, 0:3, :], in0=TI[:, 0:3, :], in1=PS[:, 4, 0:387].rearrange("p (h v) -> p h v", h=3), op=ALU.add),
             r=kTI + [("ps", 4)], w=kND)
        P.op("dve", lambda e: e.tensor_tensor(out=ND[:, 3, :], in0=TI[:, 3, :], in1=PS[:, 7, 258:387], op=ALU.add), r=kTI + [("ps", 7)], w=kND)
        P.op("act", lambda e: e.activation(out=DEN, in_=ND[:, :, 128], func=AF.Abs), r=kND, w=["den"])
        P.op("dve", lambda e: e.tensor_scalar(out=DEN, in0=DEN, scalar1=1.0, scalar2=None, op0=ALU.max), r=["den"], w=["den"])
        P.op("dve", lambda e: e.reciprocal(out=RD, in_=DEN), r=["den"], w=["rd"])
        rdb = bc(RD, [[1, 4], [0, 128]])
        hsv = HS[:, c, :].rearrange("p (h v) -> p h v", h=4)
        if arrived[c] == 0:
            P.op("dve", lambda e: e.tensor_tensor(out=hsv, in0=ND[:, :, 0:128], in1=rdb, op=ALU.mult), r=kND + ["rd"], w=kHS(c))
        else:
            P.op("dve", lambda e: e.tensor_tensor(out=HHt, in0=ND[:, :, 0:128], in1=rdb, op=ALU.mult), r=kND + ["rd"], w=kHH)
            P.op("dve", lambda e: e.tensor_tensor(out=hsv, in0=hsv, in1=HHt, op=ALU.add), r=kHH + kHS(c), w=kHS(c))
        arrived[c] += 1
        yield
        for p in range(2):
            reg = regs[p]
            bk = 2 + p
            i = d * 2 + p
            SSt, kSS = SSTs[d][p]
            for half in range(2):
                r0 = half * 64
                j = j0 + 2 * p + half
                P.op("dve", lambda e, SSt=SSt, reg=reg, r0=r0, j=j, half=half: e.scalar_tensor_tensor(
                    out=SSt[r0:r0 + 64, :], in0=SSt[r0:r0 + 64, :], scalar=EGt[r0:r0 + 64, c, j:j + 1],
                    in1=reg[r0:r0 + 64, half * 129:(half + 1) * 129], op0=ALU.mult, op1=ALU.add),
                    r=[("ps", bk)] + kSS + gk, w=kSS)
            P.op("act", lambda e, SSt=SSt, i=i: e.activation(out=SBF[0:64, i, 0:129], in_=SSt[0:64, :], func=AF.Copy), r=kSS, w=ksbf(i))
            P.op("act", lambda e, SSt=SSt, i=i: e.activation(out=SBF[64:128, i, 129:258], in_=SSt[64:128, :], func=AF.Copy), r=kSS, w=ksbf(i))
        if arrived[c] == 2:
            fin_a(c)
            pending_fin.append(c)
        yield

    seq = []
    for i in range(16):
        seq += [(i, 0), (15 - i, 1)]
    for _ in stepA(*seq[0], 0):
        pass
    for n in range(32):
        gB = stepB(*seq[n], n % 2)
        gA = stepA(*seq[n + 1], (n + 1) % 2) if n + 1 < 32 else None
        next(gB)
        if gA: next(gA)
        next(gB)
        if gA: next(gA)
        if gA: next(gA)
        next(gB)
        if gA: next(gA)
        if pending_fin:
            fin_b(pending_fin.pop(0))
        next(gB)
        if gA: next(gA)
    while pending_fin:
        fin_b(pending_fin.pop(0))

    WCs = [wtile(i * 4096, 4096) for i in range(2)]
    WMO, kWMO = wtile(8192, 8192)
    wmo_v = wmo.rearrange("(k p) f -> p k f", p=128)
    set_slots(4)
    c1seq = [(hb, g) for hb in range(4) for g in range(3)]

    def enq_c(idx):
        WCw, kWCw = WCs[idx % 2]
        enqueue(WCw, wmi_v[:, :, c1seq[idx][1] * 512:(c1seq[idx][1] + 1) * 512], kWCw)
        return WCw, kWCw

    c1info = {0: enq_c(0)}
    enqueue(WMO, wmo_v, kWMO)
    MO = R1[:, 4096:8192].rearrange("p (m t) -> p m t", m=8)

    def kMO(m):
        return kR(4096 + m * 512, 4096 + (m + 1) * 512)
    ubidx = {0: (None, 1), 1: (2, 4), 2: (3, 5), 3: (6, None)}
    wcc = 0
    pcc = 0
    XNc = [XN[:, :, 0:512], XN[:, :, 512:1024]]

    def prenorm_c1(hb):
        par = hb % 2
        prenorm(hb * 512, 512, 2, XNc[par], lambda k: kXN(k, 1024) + [("xnc", par, k)])

    prenorm_c1(0)
    for hb in range(4):
        c0 = hb * 512
        par = hb % 2
        li_, ri_ = ubidx[hb]
        lsrc = HALO[:, :, 0] if li_ is None else UB[:, :, li_]
        rsrc = HALO[:, :, 1] if ri_ is None else UB[:, :, ri_]
        allU = [k_ for i in range(4) for k_ in kU(i)]
        P.op("dve", lambda e, lsrc=lsrc: e.tensor_copy(out=U[:, :, 0], in_=lsrc), r=["halo", "ub"], w=allU)
        P.op("dve", lambda e, rsrc=rsrc: e.tensor_copy(out=U[:, :, 513], in_=rsrc), r=["halo", "ub"], w=allU)

        def conv_ct(ct):
            P.op("dve", lambda e: e.tensor_scalar(out=CTMP, in0=U[:, ct, 0:512], scalar1=CONVP[:, ct, 0:1], scalar2=None, op0=ALU.mult),
                 r=kU(ct) + ["convp"], w=kCTMP)
            P.op("dve", lambda e: e.scalar_tensor_tensor(out=CTMP, in0=U[:, ct, 1:513], scalar=CONVP[:, ct, 1:2], in1=CTMP, op0=ALU.mult, op1=ALU.add),
                 r=kU(ct) + ["convp"] + kCTMP, w=kCTMP)
            P.op("dve", lambda e: e.scalar_tensor_tensor(out=CTMP, in0=U[:, ct, 2:514], scalar=CONVP[:, ct, 2:3], in1=CTMP, op0=ALU.mult, op1=ALU.add),
                 r=kU(ct) + ["convp"] + kCTMP, w=kCTMP)
            P.op("dve", lambda e: e.scalar_tensor_tensor(out=YC[:, ct, :], in0=CTMP, scalar=CONVP[:, ct, 3:4], in1=BG[:, ct, :], op0=ALU.add, op1=ALU.mult),
                 r=kCTMP + kBG(ct) + ["convp"], w=kYC(ct))

        for g in range(3):
            idx = hb * 3 + g
            cast_all()
            if idx + 1 < 12:
                c1info[idx + 1] = enq_c(idx + 1)
            WCw, kWCw = c1info[idx]
            for mt in range(4):
                pb = pcc % 2
                pcc += 1
                for k in range(8):
                    P.op("pe", lambda e, k=k, WCw=WCw, mt=mt, pb=pb, par=par: e.matmul(PS[:, pb, :], lhsT=WCw[:, k, mt * 128:(mt + 1) * 128], rhs=XNc[par][:, k, :],
                                                                           start=(k == 0), stop=(k == 7)),
                         r=kWCw + [("xnc", par, k)], w=[("ps", pb)])
                if g == 0:
                    P.op("act", lambda e, mt=mt, pb=pb: e.activation(out=BG[:, mt, :], in_=PS[:, pb, :], func=AF.Copy), r=[("ps", pb)], w=kBG(mt))
                elif g == 1:
                    P.op("act", lambda e, mt=mt, pb=pb: e.activation(out=U[:, mt, 1:513], in_=PS[:, pb, :], func=AF.Copy), r=[("ps", pb)], w=kU(mt))
                else:
                    P.op("dve", lambda e, mt=mt, pb=pb: e.tensor_tensor(out=U[:, mt, 1:513], in0=U[:, mt, 1:513], in1=PS[:, pb, :], op=ALU.mult),
                         r=[("ps", pb)] + kU(mt), w=kU(mt))
                    conv_ct(mt)
                cast_some(2)
        if hb + 1 < 4:
            prenorm_c1(hb + 1)
        for m in range(8):
            pb = 2 + m % 2
            for k in range(8):
                rhs = YC[:, k, :] if k < 4 else QKT[:, k - 4, c0:c0 + 512]
                rk = kYC(k) if k < 4 else kQK(k - 4, c0, c0 + 512)
                P.op("pe", lambda e, k=k, m=m, pb=pb, rhs=rhs: e.matmul(PS[:, pb, :], lhsT=WMO[:, k, m * 128:(m + 1) * 128], rhs=rhs, start=(k == 0), stop=(k == 7)),
                     r=kWMO + rk, w=[("ps", pb)])
            sq = SQ[m % 2][:, 0:512]
            P.op("act", lambda e, pb=pb, sq=sq: e.activation(out=sq, in_=PS[:, pb, :], func=AF.Square), r=[("ps", pb)], w=[("sq", m % 2)])
            P.op("pe", lambda e, sq=sq, m=m: e.matmul(PS[:, 6, :], lhsT=ONESB[:], rhs=sq, start=(m == 0), stop=(m == 7)),
                 r=[("sq", m % 2), "onesb"], w=[("ps", 6)])
            P.op("dve", lambda e, pb=pb, m=m: e.tensor_scalar(out=MO[:, m, :], in0=PS[:, pb, :], scalar1=GH[:, 3, m:m + 1], scalar2=None, op0=ALU.mult),
                 r=[("ps", pb), "gh", ("sq", m % 2)], w=kMO(m))
        rstd(512)
        for m in range(8):
            P.op("dve", lambda e, m=m: e.tensor_tensor(out=MO[:, m, :], in0=MO[:, m, :], in1=STD[:, 0:512], op=ALU.mult),
                 r=kMO(m) + ["std"], w=kMO(m))
            P.op("dve", lambda e, m=m, c0=c0: e.tensor_tensor(out=X[:, m, c0:c0 + 512], in0=X[:, m, c0:c0 + 512], in1=MO[:, m, :], op=ALU.add),
                 r=kMO(m) + [xk(hb, m)], w=[xk(hb, m)])
    if stage == 2:
        for b in range(2):
            store_out(b)
        return finish()

    for b in range(2):
        ffn(b, w2i, w2o, 4, 5)
        store_out(b)
    return finish()


_CACHE = {}


def _prep(inputs, stage):
    f = lambda a: np.ascontiguousarray(np.asarray(a, dtype=np.float32))
    x = f(inputs["x"])
    gains = np.stack([f(inputs[n])[0] for n in ["norm_ffn1_pre", "norm_ffn1_post", "norm_mix_pre", "norm_mix_post",
                                                 "norm_ffn2_pre", "norm_ffn2_post"]], 0)
    gains = np.ascontiguousarray(gains.reshape(6, 8, 128).transpose(2, 0, 1))
    cw = f(inputs["conv_w"])[0]
    cb = f(inputs["conv_b"])[0]
    convp = np.concatenate([cw, cb[None, :]], 0)
    convp = np.ascontiguousarray(convp.reshape(4, 4, 128).transpose(2, 1, 0))
    row = np.concatenate([f(inputs["mlstm_norm"])[0], f(inputs["gate_i_bias"])[0], f(inputs["gate_f_bias"])[0]])
    rowp = np.ascontiguousarray(np.broadcast_to(row[None, :], (128, 528)))
    r = np.arange(128)
    masks = np.stack([(r[:, None] <= r[None, :]), (r[:, None] >= r[None, :]), (r[:, None] > r[None, :]),
                      (r[:, None] < r[None, :]), (r[:, None] == r[None, :])], 1).astype(np.float32)
    identb = np.eye(128, dtype=np.float32).astype(ml_dtypes.bfloat16)
    common = {
        "w1i": (f(inputs["w_ffn1_in"])[0][:, :512].copy() if os.environ.get("KTINY") else f(inputs["w_ffn1_in"])[0]),
        "w1o": (f(inputs["w_ffn1_out"])[0][:, :128].copy() if os.environ.get("KTINY") else f(inputs["w_ffn1_out"])[0]),
        "wmi": f(inputs["w_mix_in"])[0], "wmo": f(inputs["w_mix_out"])[0],
        "w2i": f(inputs["w_ffn2_in"])[0], "w2o": f(inputs["w_ffn2_out"])[0],
        "gains": gains, "convp": convp, "rowp": rowp, "masks": np.ascontiguousarray(masks), "identb": identb,
    }
    in_maps = []
    for c in range(NCORES):
        bidx, half = c // 2, c % 2
        def lay(hh):
            xs = x[bidx, hh * NT:(hh + 1) * NT, :]
            return np.ascontiguousarray(xs.T.reshape(8, 128, NT).transpose(1, 0, 2))
        sel = np.zeros((128, 16), np.float32)
        sel[:, 0] = 1.0 if half == 1 else 0.0
        sel[:, 1] = 1.0 if half == 0 else 0.0
        m = dict(common)
        m["xT"] = lay(half)
        m["xP"] = lay(1 - half)
        m["sel"] = sel
        in_maps.append(m)
    return in_maps


def run(inputs, stage=STAGE_FULL):
    if stage not in _CACHE:
        _CACHE[stage] = build(stage)
    nc = _CACHE[stage]
    in_maps = _prep(inputs, stage)
    res = run_bass_kernel_spmd(nc, in_maps, core_ids=list(range(NCORES)))
    out = np.empty((4, 4096, D), np.float32)
    for c in range(NCORES):
        yt = np.asarray(res.results[c]["yT"])
        out[c // 2, (c % 2) * NT:(c % 2 + 1) * NT, :] = yt.transpose(1, 0, 2).reshape(D, NT).T
    return out


def kernel(**inputs):
    return run(inputs, STAGE_FULL)
```

```python
import os
import numpy as np
import ml_dtypes
from contextlib import ExitStack
import concourse.bass as bass
import concourse.mybir as mybir
from concourse.bass_utils import run_bass_kernel_spmd

F32 = mybir.dt.float32
BF16 = mybir.dt.bfloat16
AF = mybir.ActivationFunctionType
ALU = mybir.AluOpType
AX = mybir.AxisListType

D = 1024
DFF = 2816
NT = 2048
NCORES = 8
EPS = 1e-6
BIG = 60.0
STAGE_FULL = 9


class Op:
    __slots__ = ("eng", "fn", "deps", "slot", "sig", "need", "idx")


class Prog:
    def __init__(self):
        self.ops = []
        self.lw = {}
        self.rd = {}

    def op(self, eng, fn, r=(), w=(), slot=None):
        o = Op()
        o.eng, o.fn, o.slot, o.sig, o.need, o.idx = eng, fn, slot, None, False, len(self.ops)
        deps = {}
        for k in r:
            d = self.lw.get(k)
            if d is not None:
                deps[d.idx] = d
        for k in w:
            d = self.lw.get(k)
            if d is not None:
                deps[d.idx] = d
            for d in self.rd.get(k, ()):
                deps[d.idx] = d
        for k in r:
            self.rd.setdefault(k, []).append(o)
        for k in w:
            self.lw[k] = o
            self.rd[k] = []
        o.deps = []
        for d in deps.values():
            if d.slot is None and d.eng == eng and eng == "pe":
                continue
            d.need = True
            o.deps.append(d)
        self.ops.append(o)
        return o

    def emit(self, nc, es):
        engs = ["pe", "act", "dve", "pool", "sp"]
        sems = {e: es.enter_context(nc.semaphore("s_" + e)) for e in engs}
        slots = {}
        cnt = {e: 0 for e in engs}
        scnt = {}
        for o in self.ops:
            if o.slot is not None:
                if o.slot not in slots:
                    slots[o.slot] = es.enter_context(nc.semaphore("d_" + o.slot))
                    scnt[o.slot] = 0
                scnt[o.slot] += 16
                o.sig = (slots[o.slot], scnt[o.slot], 16)
            elif o.need:
                cnt[o.eng] += 1
                o.sig = (sems[o.eng], cnt[o.eng], 1)
        streams = {e: [o for o in self.ops if o.eng == e] for e in engs}

        def run(ename, e):
            waited = {}
            for o in streams[ename]:
                for d in o.deps:
                    s, v, _ = d.sig
                    if waited.get(s.num, 0) < v:
                        e.wait_ge(s, v)
                        waited[s.num] = v
                ins = o.fn(e)
                if o.sig is not None and ins is not None:
                    ins.then_inc(o.sig[0], o.sig[2])

        block = es.enter_context(nc.Block())
        block.tensor(lambda e: run("pe", e))
        block.scalar(lambda e: run("act", e))
        block.vector(lambda e: run("dve", e))
        block.gpsimd(lambda e: run("pool", e))
        block.sync(lambda e: run("sp", e))


def bc(ap, pattern):
    return bass.AP(ap.tensor, ap.offset, [list(ap.ap[0])] + [list(x) for x in pattern])


def build(stage=STAGE_FULL):
    nc = bass.Bass("TRN2", target_bir_lowering=False)
    es = ExitStack()
    P = Prog()

    def din(name, shape, dt=F32):
        return nc.dram_tensor(name, shape, dt, kind="ExternalInput").ap()

    xT = din("xT", [128, 8, NT])
    TINY = bool(os.environ.get("KTINY"))
    if TINY:
        w1i = din("w1i", [D, 512]); w1o = din("w1o", [DFF, 128])
    else:
        w1i = din("w1i", [D, 2 * DFF]); w1o = din("w1o", [DFF, D])
    wmi = din("wmi", [D, 3088]); wmo = din("wmo", [D, D])
    w2i = din("w2i", [D, 2 * DFF]); w2o = din("w2o", [DFF, D])
    gains_d = din("gains", [128, 6, 8])
    convp_d = din("convp", [128, 4, 4])
    rowp_d = din("rowp", [128, 528])
    masks_d = din("masks", [128, 5, 128])
    identb_d = din("identb", [128, 128], BF16)
    sel_d = din("sel", [128, 16])
    yT = nc.dram_tensor("yT", [128, 8, NT], F32, kind="ExternalOutput").ap()
    FUSED = stage >= 2
    if FUSED:
        xP = din("xP", [128, 8, NT])
        pay_d = nc.dram_tensor("pay", [128, 524], F32).ap()

    def sb(name, shape, dt):
        return es.enter_context(nc.sbuf_tensor(name, shape, dt))

    X = sb("X", [128, 8, NT], F32)
    R1 = sb("R1", [128, 8192], F32)
    Hreg = sb("H", [128, 24832], BF16)
    Wreg = sb("W", [128, 16384], BF16)
    Mreg = sb("M", [128, 3072], F32)
    Greg = sb("G", [128, 1880], F32)
    STG = sb("STG", [128, 2, 512], F32)
    GAINS = sb("gains_s", [128, 6, 8], F32)
    GH = sb("gh_s", [128, 6, 8], F32)
    CONVP = sb("convp_s", [128, 4, 4], F32)
    ROWP = sb("rowp_s", [128, 528], F32)
    MASKS = sb("masks_s", [128, 5, 128], F32)
    IDENTB = sb("identb_s", [128, 128], BF16)
    SEL = sb("sel_s", [128, 16], F32)
    ONESB = sb("onesb", [128, 128], BF16)
    ONESF = sb("onesf", [128, 128], F32)
    CST = sb("cst", [128, 4], F32)
    PS = es.enter_context(nc.psum_tensor("PS", [128, 8, 512], F32))

    R1b = R1[:].bitcast(BF16)
    XN = R1b[:, 0:8192].rearrange("p (k t) -> p k t", k=8)
    HO = R1[:].rearrange("p (m t) -> p m t", m=8)
    Hh = Hreg[:, 0:22528].rearrange("p (f t) -> p f t", f=22)
    QKT = Hreg[:, 0:8192].rearrange("p (m t) -> p m t", m=4)
    VAUG = Hreg[:, 8192:16448].rearrange("p (c h v) -> p c h v", c=16, h=4)
    SGO = Hreg[:, 16448:24640].rearrange("p (c f) -> p c f", c=16)
    Hf = Hreg[:].bitcast(F32)
    BG = Hf[:, 4096:6144].rearrange("p (c t) -> p c t", c=4)
    U = Hf[:, 6144:8200].rearrange("p (c t) -> p c t", c=4)
    YC = Hreg[:, 16400:18448].rearrange("p (c t) -> p c t", c=4)
    CTMP = Hf[:, 9300:9812]
    Wb = Wreg[:]
    Wf = Wreg[:].bitcast(F32)
    SQ = [Mreg[:, 0:512].bitcast(BF16), Mreg[:, 512:1024].bitcast(BF16)]
    STD = Mreg[:, 1024:2048]
    SIL = [Mreg[:, 2048:2560], Mreg[:, 2560:3072]]

    def G3(off, a, b):
        return Greg[:, off:off + a * b].rearrange("p (a b) -> p a b", a=a)
    GR = G3(0, 16, 16)
    LI = G3(256, 16, 8); LF = G3(384, 16, 8); ZZ = G3(512, 16, 8); AZ = G3(640, 16, 8)
    BC = G3(768, 16, 8); TOT = G3(896, 16, 8); EW = G3(1024, 16, 8); EB = G3(1152, 16, 8)
    EGt = G3(1280, 16, 8); GAF = G3(1408, 16, 8); EWA = G3(1536, 16, 8); LIB = G3(1664, 16, 8)
    UB = G3(1792, 4, 8)
    HALO = G3(1824, 4, 2)
    UBT = Greg[:, 1832:1864]
    DEN = Greg[:, 1864:1868]; RD = Greg[:, 1868:1872]; SSH = Greg[:, 1872:1876]


    EPS_AP = CST[:, 0:1]; NBIG_AP = CST[:, 1:2]; ONE_AP = CST[:, 2:3]

    def kH(lo, hi):
        return [("H", g) for g in range(lo // 128, (hi - 1) // 128 + 1)]

    def kW(lo, hi):
        return [("W", g) for g in range(lo // 256, (hi - 1) // 256 + 1)]

    def kR(lo, hi):
        return [("R1", g) for g in range(lo // 512, (hi - 1) // 512 + 1)]

    def kXN(k, n=1024):
        return kR(k * 512, k * 512 + n // 2)

    def kHO(m):
        return kR(m * 1024, (m + 1) * 1024)

    def kHh(f, hf):
        return kH(f * 1024 + hf * 512, f * 1024 + hf * 512 + 512)

    def kQK(m, c0, c1):
        return kH(m * 2048 + c0, m * 2048 + c1)

    def kQKall(c0, c1):
        return [k for m in range(4) for k in kQK(m, c0, c1)]

    def kVA(c):
        return kH(8192 + c * 516, 8192 + (c + 1) * 516)

    def kSGO(c):
        return kH(16448 + c * 512, 16448 + (c + 1) * 512)

    def kBG(mt):
        return kH(8192 + mt * 1024, 8192 + (mt + 1) * 1024)

    def kU(mt):
        return kH(12288 + mt * 1028, 12288 + (mt + 1) * 1028)

    def kYC(ct):
        return kH(16400 + ct * 512, 16400 + (ct + 1) * 512)

    kCTMP = kH(18600, 19624)

    def dma(eng, out, in_, r, w, slot):
        return P.op(eng, lambda e: e.dma_start(out=out, in_=in_), r=r, w=w, slot=slot)

    def xk(hb, k):
        return ("x", hb, k)

    STGS = [(STG[:, 0, :], [("stg", 0)]), (STG[:, 1, :], [("stg", 1)]),
            (Hf[:, 11264:11776], [("stg", 2)] + kH(22528, 23552)), (Hf[:, 11776:12288], [("stg", 3)] + kH(23552, 24576))]
    wq = {"pend": [], "infl": [], "n": 0, "nslots": 2, "ce": 0}

    def set_slots(n):
        assert not wq["pend"] and not wq["infl"]
        wq["nslots"] = n
        wq["n"] = 0

    def pump():
        while wq["pend"] and len(wq["infl"]) < wq["nslots"]:
            dst_p, src_p, keys, shp = wq["pend"].pop(0)
            sl = wq["n"] % wq["nslots"]
            wq["n"] += 1
            stf, sk = STGS[sl]
            st = stf[:, 0:shp[0] * shp[1]].rearrange("p (a b) -> p a b", a=shp[0])
            P.op("sp", lambda e, st=st, src_p=src_p: e.dma_start(out=st, in_=src_p), r=[], w=sk, slot="stg%d" % sl)
            wq["infl"].append((dst_p, st, sk, keys))

    def enqueue(dst, src, keys):
        A, B = dst.shape[1], dst.shape[2]
        Bc = min(B, 512)
        ra = max(1, 512 // Bc)
        for b0 in range(0, B, Bc):
            for a0 in range(0, A, ra):
                r_ = min(ra, A - a0)
                wq["pend"].append((dst[:, a0:a0 + r_, b0:b0 + Bc], src[:, a0:a0 + r_, b0:b0 + Bc], keys, (r_, Bc)))
        pump()

    def cast_some(n):
        for _ in range(n):
            if not wq["infl"]:
                pump()
            if not wq["infl"]:
                return
            dst_p, st, sk, keys = wq["infl"].pop(0)
            wq["ce"] += 1
            if wq["ce"] % 2:
                P.op("act", lambda e, dst_p=dst_p, st=st: e.activation(out=dst_p, in_=st, func=AF.Copy), r=sk, w=keys)
            else:
                P.op("dve", lambda e, dst_p=dst_p, st=st: e.tensor_copy(out=dst_p, in_=st), r=sk, w=keys)
            pump()

    def cast_all():
        while wq["pend"] or wq["infl"]:
            cast_some(1)

    def wload(dst, src, keys):
        enqueue(dst, src, keys)
        cast_all()

    def load_x(src, eng):
        for b in range(2):
            for k in range(8):
                dma(eng, X[:, k, b * 1024:(b + 1) * 1024], src[:, k, b * 1024:(b + 1) * 1024], [], [xk(2 * b, k), xk(2 * b + 1, k)],
                    "xin%d_%d" % (k, b))

    dma("sp", GAINS[:], gains_d, [], ["gains"], "c0")
    dma("sp", CONVP[:], convp_d, [], ["convp"], "c1")
    dma("sp", ROWP[:], rowp_d, [], ["rowp"], "c2")
    dma("sp", MASKS[:], masks_d, [], ["masks"], "c3")
    dma("sp", IDENTB[:], identb_d, [], ["identb"], "c4")
    dma("sp", SEL[:], sel_d, [], ["sel"], "c5")
    load_x((xP if FUSED else xT), "sp")
    P.op("dve", lambda e: e.memset(ONESB[:], 1.0), w=["onesb"])
    P.op("dve", lambda e: e.memset(ONESF[:], 1.0), w=["onesf"])
    P.op("dve", lambda e: e.memset(CST[:, 0:1], EPS), w=["cst"])
    P.op("dve", lambda e: e.memset(CST[:, 1:2], -BIG), w=["cst"])
    P.op("dve", lambda e: e.memset(CST[:, 2:3], 1.0), w=["cst"])
    for gi, sc in enumerate([1.0, 0.5, 1.0, 1.0, 1.0, 0.5]):
        P.op("dve", lambda e, gi=gi, sc=sc: e.tensor_scalar(out=GH[:, gi, :], in0=GAINS[:, gi, :], scalar1=sc,
                                                          scalar2=None, op0=ALU.mult), r=["gains"], w=["gh"])

    def prenorm(c0, n, gi, xv=None, xkeys=None):
        if xv is None:
            xv = XN[:, :, 0:n]
            xkeys = lambda k: kXN(k, n)
        nb = n // 512
        hbs = [c0 // 512 + i for i in range(nb)]
        for k in range(8):
            sq = SQ[k % 2][:, 0:n]
            P.op("act", lambda e, k=k, sq=sq: e.activation(out=sq, in_=X[:, k, c0:c0 + n], func=AF.Square),
                 r=[xk(h, k) for h in hbs], w=[("sq", k % 2)])
            for hf in range(nb):
                P.op("pe", lambda e, k=k, sq=sq, hf=hf: e.matmul(PS[:, 6 + hf, :], lhsT=ONESB[:], rhs=sq[:, hf * 512:(hf + 1) * 512],
                                                              start=(k == 0), stop=(k == 7)),
                     r=[("sq", k % 2), "onesb"], w=[("ps", 6 + hf)])
        rstd(n)
        for k in range(8):
            P.op("dve", lambda e, k=k: e.scalar_tensor_tensor(out=xv[:, k, :], in0=X[:, k, c0:c0 + n], scalar=GAINS[:, gi, k:k + 1],
                                                            in1=STD[:, 0:n], op0=ALU.mult, op1=ALU.mult),
                 r=[xk(h, k) for h in hbs] + ["std", "gains"], w=xkeys(k))

    def rstd(n):
        nb = n // 512
        P.op("act", lambda e: e.activation(out=STD[:, 0:n].rearrange("p (a b) -> p a b", a=nb), in_=PS[:, 6:6 + nb, :], func=AF.Ln,
                                          scale=1.0 / D, bias=EPS_AP),
             r=[("ps", 6 + i) for i in range(nb)] + ["cst"], w=["std"])
        P.op("act", lambda e: e.activation(out=STD[:, 0:n], in_=STD[:, 0:n], func=AF.Exp, scale=-0.5), r=["std"], w=["std"])

    wctr = {"i": 0, "o": 0}

    def ffn(b, w_in, w_out, gpre, gpost, dbg=None):
        c0 = b * 1024
        w_in_v = w_in.rearrange("(k p) f -> p k f", p=128)
        w_out_v = w_out.rearrange("(f p) m -> p f m", p=128)
        set_slots(4)
        prenorm(c0, 1024, gpre)
        pcnt = 0

        def enq_in(grp):
            bi = wctr["i"] % 2
            wctr["i"] += 1
            WGt = Wb[:, bi * 4096:bi * 4096 + 2048].rearrange("p (k f) -> p k f", k=8)
            WUt = Wb[:, bi * 4096 + 2048:bi * 4096 + 4096].rearrange("p (k f) -> p k f", k=8)
            kg = kW(bi * 4096, bi * 4096 + 2048)
            ku = kW(bi * 4096 + 2048, bi * 4096 + 4096)
            gsl = slice(0, 256) if TINY else slice(grp * 256, (grp + 1) * 256)
            usl = slice(256, 512) if TINY else slice(DFF + grp * 256, DFF + (grp + 1) * 256)
            enqueue(WGt, w_in_v[:, :, gsl], kg)
            enqueue(WUt, w_in_v[:, :, usl], ku)
            return WGt, WUt, kg, ku

        def enq_out(m):
            bo = wctr["o"] % 2
            wctr["o"] += 1
            WOt = Wb[:, 8192 + bo * 2816:8192 + (bo + 1) * 2816].rearrange("p (f m) -> p f m", f=22)
            ko = kW(8192 + bo * 2816, 8192 + (bo + 1) * 2816)
            enqueue(WOt, w_out_v[:, :, (slice(0, 128) if TINY else slice(m * 128, (m + 1) * 128))], ko)
            return WOt, ko

        ginfo = {0: enq_in(0)}
        cast_all()
        oinfo = {}
        for grp in range(11):
            if grp + 1 < 11:
                ginfo[grp + 1] = enq_in(grp + 1)
            else:
                oinfo[0] = enq_out(0)
            WGt, WUt, kg, ku = ginfo[grp]
            for hf in range(2):
                for fi in range(2):
                    f = grp * 2 + fi
                    gb = pcnt % 2
                    ub = 2 + pcnt % 2
                    s = pcnt % 2
                    pcnt += 1
                    for k in range(8):
                        P.op("pe", lambda e, k=k, gb=gb, fi=fi, hf=hf, WGt=WGt: e.matmul(
                            PS[:, gb, :], lhsT=WGt[:, k, fi * 128:(fi + 1) * 128], rhs=XN[:, k, hf * 512:(hf + 1) * 512],
                            start=(k == 0), stop=(k == 7)), r=kg + kXN(k), w=[("ps", gb)])
                    for k in range(8):
                        P.op("pe", lambda e, k=k, ub=ub, fi=fi, hf=hf, WUt=WUt: e.matmul(
                            PS[:, ub, :], lhsT=WUt[:, k, fi * 128:(fi + 1) * 128], rhs=XN[:, k, hf * 512:(hf + 1) * 512],
                            start=(k == 0), stop=(k == 7)), r=ku + kXN(k), w=[("ps", ub)])
                    P.op("act", lambda e, gb=gb, s=s: e.activation(out=SIL[s], in_=PS[:, gb, :], func=AF.Silu),
                         r=[("ps", gb)], w=[("sil", s)])
                    P.op("dve", lambda e, ub=ub, s=s, f=f, hf=hf: e.tensor_tensor(
                        out=Hh[:, f, hf * 512:(hf + 1) * 512], in0=SIL[s], in1=PS[:, ub, :], op=ALU.mult),
                        r=[("sil", s), ("ps", ub)], w=kHh(f, hf))
                    cast_some(2)
            cast_all()
        if dbg == 0.6:
            return
        for m in range(8):
            if m + 1 < 8:
                oinfo[m + 1] = enq_out(m + 1)
                cast_all()
            WOt, ko = oinfo[m]
            ob = [4, 0, 2][m % 3]
            for f in range(22):
                for hf in range(2):
                    P.op("pe", lambda e, f=f, hf=hf, ob=ob, WOt=WOt: e.matmul(
                        PS[:, ob + hf, :], lhsT=WOt[:, f, :], rhs=Hh[:, f, hf * 512:(hf + 1) * 512],
                        start=(f == 0), stop=(f == 21)), r=ko + kHh(f, hf), w=[("ps", ob + hf)])
            sq = SQ[m % 2]
            if dbg == 0.61:
                P.op("act", lambda e, ob=ob, m=m: e.activation(out=HO[:, m, :].rearrange("p (a b) -> p a b", a=2), in_=PS[:, ob:ob + 2, :],
                                                             func=AF.Copy), r=[("ps", ob), ("ps", ob + 1)], w=kHO(m))
                continue
            P.op("act", lambda e, ob=ob, sq=sq: e.activation(out=sq.rearrange("p (a b) -> p a b", a=2), in_=PS[:, ob:ob + 2, :],
                                                           func=AF.Square), r=[("ps", ob), ("ps", ob + 1)], w=[("sq", m % 2)])
            for hf in range(2):
                P.op("pe", lambda e, sq=sq, hf=hf, m=m: e.matmul(PS[:, 6 + hf, :], lhsT=ONESB[:], rhs=sq[:, hf * 512:(hf + 1) * 512],
                                                              start=(m == 0), stop=(m == 7)),
                     r=[("sq", m % 2), "onesb"], w=[("ps", 6 + hf)])
            P.op("dve", lambda e, ob=ob, m=m: e.tensor_scalar(out=HO[:, m, :].rearrange("p (a b) -> p a b", a=2), in0=PS[:, ob:ob + 2, :],
                                                            scalar1=GH[:, gpost, m:m + 1], scalar2=None, op0=ALU.mult),
                 r=[("ps", ob), ("ps", ob + 1), "gh", ("sq", m % 2)], w=kHO(m))
        if dbg in (0.61, 0.65):
            return
        rstd(1024)
        for m in range(8):
            P.op("dve", lambda e, m=m: e.tensor_tensor(out=HO[:, m, :], in0=HO[:, m, :], in1=STD[:, 0:1024], op=ALU.mult),
                 r=kHO(m) + ["std"], w=kHO(m))
            P.op("dve", lambda e, m=m: e.tensor_tensor(out=X[:, m, c0:c0 + 1024], in0=X[:, m, c0:c0 + 1024], in1=HO[:, m, :], op=ALU.add),
                 r=kHO(m) + [xk(2 * b, m), xk(2 * b + 1, m)], w=[xk(2 * b, m), xk(2 * b + 1, m)])
        set_slots(2)

    def store_out(b):
        for k in range(8):
            dma("sp", yT[:, k, b * 1024:(b + 1) * 1024], X[:, k, b * 1024:(b + 1) * 1024], [xk(2 * b, k), xk(2 * b + 1, k)],
                [("y", b, k)], "yo%d_%d" % (b, k))

    def finish():
        keys = list(P.lw.keys())
        P.op("sp", lambda e: None, r=[k for k in keys if isinstance(k, tuple) and k[0] == "y"])
        P.emit(nc, es)
        es.close()
        return nc

    if stage == 0:
        for b in range(2):
            store_out(b)
        return finish()
    if stage == 0.5:
        prenorm(0, 1024, 0)
        for b in range(2):
            store_out(b)
        return finish()
    for b in range(2):
        ffn(b, w1i, w1o, 0, 1, stage)
        if stage in (0.6, 0.61, 0.65, 0.7):
            break
    if stage <= 1:
        for b in range(2):
            store_out(b)
        return finish()

    wmi_v = wmi.rearrange("(k p) f -> p k f", p=128)

    def wtile(lo, n, kdim=8):
        return Wb[:, lo:lo + n].rearrange("p (k f) -> p k f", k=kdim), kW(lo, lo + n)

    WQK, kWQK = wtile(0, 4096)
    WV, kWV = wtile(4096, 4096)
    WOG, kWOG = wtile(8192, 4096)
    WGA, kWGA = wtile(12288, 128)
    WBt = [wtile(12544 + i * 1024, 1024) for i in range(2)]
    XB, kXB = wtile(14592, 32)
    GROW = ROWP[:, 0:512]
    BI = bc(ROWP[:, 512:520], [[0, 16], [1, 8]])
    BF_ = bc(ROWP[:, 520:528], [[0, 16], [1, 8]])
    gk = ["gates"]
    RQf = Wb[:, 13568:14080]; kRQ = kW(13568, 14080)
    RQ = RQf.rearrange("p (a b) -> p a b", a=2)
    SBFf = Wb[:, 14080:15112]; kSBF = kW(14080, 15112)
    SBF = SBFf.rearrange("p (a b) -> p a b", a=4)

    def ksbf(i):
        return kSBF

    def ftile(lo, n):
        return Wf[:, lo:lo + n], kW(2 * lo, 2 * (lo + n))

    def btile(lo, n):
        return Wb[:, lo:lo + n], kW(lo, lo + n)

    E_t, kE = ftile(0, 512)
    L1f, kL1 = ftile(512, 512); L1 = L1f.rearrange("p (h s) -> p h s", h=4)
    L2f, kL2 = ftile(1024, 512); L2 = L2f.rearrange("p (h s) -> p h s", h=4)
    TIf, kTI = ftile(1536, 516); TI = TIf.rearrange("p (h v) -> p h v", h=4)
    NDf, kND = ftile(2052, 516); ND = NDf.rearrange("p (h v) -> p h v", h=4)
    HHf, kHH = ftile(2568, 512); HHt = HHf.rearrange("p (h v) -> p h v", h=4)
    SQT, kSQT = ftile(3080, 512)
    PT, kPT = btile(7184, 512)
    KT2f, kKT2 = btile(7696, 256); KT2 = KT2f.rearrange("p (h k) -> p h k", h=4)
    YTOK, kYTOK = btile(8208, 512)
    PT_b, kPT_b = btile(15360, 512)
    KT2f_b, kKT2_b = btile(15872, 256); KT2_b = KT2f_b.rearrange("p (h k) -> p h k", h=4)
    PTs = [(PT, kPT), (PT_b, kPT_b)]
    KT2s = [(KT2, kKT2), (KT2_b, kKT2_b)]
    KTAs = [btile(9232 + i * 512, 512) for i in range(2)]
    EXS, kEXS = ftile(5128, 524)
    SSTs = [[ftile(4736 + (d * 2 + p) * 256, 129) for p in range(2)] for d in range(2)]
    ACCS, kACCS = ftile(6200, 524)
    EXG = R1[:, 0:4192].rearrange("p (r c) -> p r c", r=8)
    kEXG = kR(0, 4192)

    def ktok_transposes(c, bank):
        cs = slice(c * 128, (c + 1) * 128)
        kb = PS[:, bank, 0:128].bitcast(BF16)
        for p in range(2):
            P.op("pe", lambda e, p=p, kb=kb, cs=cs: e.transpose(out=kb[:, p * 128:(p + 1) * 128], in_=QKT[:, 2 + p, cs], identity=IDENTB[:]),
                 r=kQK(2 + p, c * 128, (c + 1) * 128) + ["identb"], w=[("ps", bank)])
        return kb


    def mixer_front(do_pass1):
        set_slots(4 if do_pass1 else 2)
        enqueue(WQK, wmi_v[:, :, 1536:2048], kWQK)
        enqueue(WV, wmi_v[:, :, 2048:2560], kWV)
        if not do_pass1:
            enqueue(WOG, wmi_v[:, :, 2560:3072], kWOG)
        enqueue(WGA, wmi_v[:, :, 3072:3088], kWGA)
        P.op("dve", lambda e: e.memset(VAUG[:, :, :, 128:129], 1.0), w=[k for c in range(16) for k in kVA(c)])
        cnt2 = 0
        wbc = 0
        def enq_wb(m8):
            nonlocal wbc
            WBw, kWBw = WBt[wbc % 2]
            wbc += 1
            enqueue(WBw, wmi_v[:, :, 512 + m8 * 128:512 + (m8 + 1) * 128], kWBw)
            return WBw, kWBw

        XNs = [XN, R1b[:, 8192:16384].rearrange("p (k t) -> p k t", k=8)]
        kXNs = [lambda k: kXN(k), lambda k: kR(4096 + k * 512, 4096 + (k + 1) * 512)]
        prenorm(0, 1024, 2, XNs[0], kXNs[0])
        cast_all()
        prenorm(1024, 1024, 2, XNs[1], kXNs[1])
        if do_pass1:
            load_x(xT, "pool")
        for b in range(2):
            XNb, kXNb = XNs[b], kXNs[b]
            wbi = {0: enq_wb(0)}
            for m in range(2 if do_pass1 else 0, 4):
                for hf in range(2):
                    pb = cnt2 % 2
                    cnt2 += 1
                    for k in range(8):
                        P.op("pe", lambda e, XNb=XNb, k=k, m=m, hf=hf, pb=pb: e.matmul(PS[:, pb, :], lhsT=WQK[:, k, m * 128:(m + 1) * 128],
                                                                           rhs=XNb[:, k, hf * 512:(hf + 1) * 512], start=(k == 0), stop=(k == 7)),
                             r=kWQK + kXNb(k), w=[("ps", pb)])
                    cs = b * 1024 + hf * 512
                    P.op("act", lambda e, m=m, pb=pb, cs=cs: e.activation(out=QKT[:, m, cs:cs + 512], in_=PS[:, pb, :], func=AF.Copy,
                                                                         scale=(0.125 if m < 2 else 1.0)),
                         r=[("ps", pb)], w=kQK(m, cs, cs + 512))
            for cc in range(8):
                c = b * 8 + cc
                tc_ = slice(cc * 128, (cc + 1) * 128)
                vb = 2 + cc % 2
                obk = 4 + cc % 2
                for k in range(8):
                    P.op("pe", lambda e, XNb=XNb, k=k, tc_=tc_, vb=vb: e.matmul(PS[:, vb, :], lhsT=XNb[:, k, tc_], rhs=WV[:, k, :], start=(k == 0), stop=(k == 7)),
                         r=kWV + kXNb(k), w=[("ps", vb)])
                P.op("dve", lambda e, c=c, vb=vb: e.tensor_copy(out=VAUG[:, c, :, 0:128], in_=PS[:, vb, :].rearrange("p (h v) -> p h v", h=4)),
                     r=[("ps", vb)], w=kVA(c))
                if not do_pass1:
                    for k in range(8):
                        P.op("pe", lambda e, XNb=XNb, k=k, tc_=tc_, obk=obk: e.matmul(PS[:, obk, :], lhsT=XNb[:, k, tc_], rhs=WOG[:, k, :], start=(k == 0), stop=(k == 7)),
                             r=kWOG + kXNb(k), w=[("ps", obk)])
                    P.op("act", lambda e, obk=obk: e.activation(out=SIL[0], in_=PS[:, obk, :], func=AF.Sigmoid), r=[("ps", obk)], w=[("sil", 0)])
                    P.op("dve", lambda e, c=c: e.tensor_tensor(out=SGO[:, c, :], in0=SIL[0], in1=GROW, op=ALU.mult),
                         r=[("sil", 0), "rowp"], w=kSGO(c))
                for k in range(8):
                    P.op("pe", lambda e, XNb=XNb, k=k, tc_=tc_, cc=cc: e.matmul(PS[:, 7, cc * 16:(cc + 1) * 16], lhsT=XNb[:, k, tc_], rhs=WGA[:, k, :],
                                                                    start=(k == 0), stop=(k == 7)),
                         r=kWGA + kXNb(k), w=[("ps", 7)])
            P.op("dve", lambda e, XNb=XNb, b=b: e.tensor_copy(out=GR[:, b * 8:(b + 1) * 8, :], in_=PS[:, 7, 0:128].rearrange("p (c g) -> p c g", c=8)),
                 r=[("ps", 7)], w=["gr"])
            allxn = [k_ for k in range(8) for k_ in kXNb(k)]
            P.op("dve", lambda e, XNb=XNb: e.tensor_copy(out=XB[:, :, 0:2], in_=bc(XNb[:, :, 0:1], [[1024, 8], [512, 2]])), r=allxn, w=kXB)
            P.op("dve", lambda e, XNb=XNb: e.tensor_copy(out=XB[:, :, 2:4], in_=bc(XNb[:, :, 511:512], [[1024, 8], [512, 2]])), r=allxn, w=kXB)
            for m8 in range(8):
                cast_all()
                if m8 + 1 < 8:
                    wbi[m8 + 1] = enq_wb(m8 + 1)
                WBw, kWBw = wbi[m8]
                for k in range(8):
                    P.op("pe", lambda e, k=k, m8=m8, WBw=WBw: e.matmul(PS[:, 6, m8 * 4:(m8 + 1) * 4], lhsT=WBw[:, k, :], rhs=XB[:, k, :],
                                                                    start=(k == 0), stop=(k == 7)),
                         r=kWBw + kXB, w=[("ps", 6)])
            P.op("act", lambda e: e.activation(out=UBT[:, 0:16], in_=PS[:, 6, 0:16], func=AF.Copy), r=[("ps", 6)], w=["ubt"])
            P.op("dve", lambda e, b=b: e.tensor_tensor(out=UB[:, :, b * 4:(b + 1) * 4], in0=UBT[:, 0:16].rearrange("p (c t) -> p c t", c=4),
                                                     in1=PS[:, 6, 16:32].rearrange("p (c t) -> p c t", c=4), op=ALU.mult),
                 r=["ubt", ("ps", 6)], w=["ub"])

        P.op("dve", lambda e: e.tensor_tensor(out=LI, in0=GR[:, :, 0:8], in1=BI, op=ALU.add), r=["gr", "rowp"], w=gk)
        P.op("dve", lambda e: e.tensor_tensor(out=ZZ, in0=GR[:, :, 8:16], in1=BF_, op=ALU.add), r=["gr", "rowp"], w=gk)
        P.op("act", lambda e: e.activation(out=AZ, in_=ZZ, func=AF.Abs), r=gk, w=gk)
        P.op("act", lambda e: e.activation(out=AZ, in_=AZ, func=AF.Exp, scale=-1.0), r=gk, w=gk)
        P.op("act", lambda e: e.activation(out=AZ, in_=AZ, func=AF.Ln, bias=ONE_AP), r=gk + ["cst"], w=gk)
        P.op("dve", lambda e: e.tensor_scalar(out=ZZ, in0=ZZ, scalar1=0.0, scalar2=None, op0=ALU.min), r=gk, w=gk)
        P.op("dve", lambda e: e.tensor_tensor(out=LF, in0=ZZ, in1=AZ, op=ALU.subtract), r=gk, w=gk)
        P.op("dve", lambda e: e.tensor_scalar(out=LIB, in0=LI, scalar1=BIG, scalar2=None, op0=ALU.add), r=gk, w=gk)
        P.op("pe", lambda e: e.matmul(PS[:, 0, 0:64], lhsT=MASKS[:, 0, :], rhs=LF[:, :, 0:4], start=True, stop=True),
             r=gk + ["masks"], w=[("ps", 0)])
        P.op("pe", lambda e: e.matmul(PS[:, 0, 64:128], lhsT=MASKS[:, 1, :], rhs=LF[:, :, 4:8], start=True, stop=True),
             r=gk + ["masks"], w=[("ps", 0)])
        P.op("pe", lambda e: e.matmul(PS[:, 0, 128:256], lhsT=ONESF[:], rhs=LF.rearrange("p c j -> p (c j)"), start=True, stop=True),
             r=gk + ["onesf"], w=[("ps", 0)])
        P.op("dve", lambda e: e.tensor_copy(out=BC[:, :, 0:4], in_=PS[:, 0, 0:64].rearrange("p (c j) -> p c j", c=16)), r=[("ps", 0)], w=gk)
        P.op("dve", lambda e: e.tensor_copy(out=BC[:, :, 4:8], in_=PS[:, 0, 64:128].rearrange("p (c j) -> p c j", c=16)), r=[("ps", 0)], w=gk)
        P.op("dve", lambda e: e.tensor_copy(out=TOT, in_=PS[:, 0, 128:256].rearrange("p (c j) -> p c j", c=16)), r=[("ps", 0)], w=gk)
        P.op("dve", lambda e: e.tensor_tensor(out=EW, in0=TOT, in1=BC, op=ALU.subtract), r=gk, w=gk)
        P.op("dve", lambda e: e.tensor_tensor(out=EW, in0=EW, in1=LI, op=ALU.add), r=gk, w=gk)
        P.op("dve", lambda e: e.memset(GAF, 0.0), r=gk, w=gk)
        for c in range(14, -1, -1):
            P.op("dve", lambda e, c=c: e.tensor_tensor(out=GAF[:, c, 0:4], in0=GAF[:, c + 1, 0:4], in1=TOT[:, c + 1, 0:4], op=ALU.add), r=gk, w=gk)
        for c in range(1, 16):
            P.op("dve", lambda e, c=c: e.tensor_tensor(out=GAF[:, c, 4:8], in0=GAF[:, c - 1, 4:8], in1=TOT[:, c - 1, 4:8], op=ALU.add), r=gk, w=gk)
        P.op("dve", lambda e: e.tensor_tensor(out=EWA, in0=EW, in1=GAF, op=ALU.add), r=gk, w=gk)
        P.op("act", lambda e: e.activation(out=EWA, in_=EWA, func=AF.Exp), r=gk, w=gk)
        P.op("act", lambda e: e.activation(out=EW, in_=EW, func=AF.Exp), r=gk, w=gk)
        P.op("act", lambda e: e.activation(out=EB, in_=BC, func=AF.Exp), r=gk, w=gk)
        P.op("act", lambda e: e.activation(out=EGt, in_=TOT, func=AF.Exp), r=gk, w=gk)

        if do_pass1:
            for c in range(16):
                bank = 4 + c % 2
                kb = ktok_transposes(c, bank)
                KTAf, kKTA = KTAs[c % 2]
                KTA = KTAf.rearrange("p (d h k) -> p d h k", d=2, h=4)
                P.op("dve", lambda e, KTA=KTA, kb=kb, c=c: e.tensor_tensor(
                    out=KTA, in0=bc(kb, [[0, 2], [64, 4], [1, 64]]), in1=bc(EWA[:, c, :], [[4, 2], [1, 4], [0, 64]]), op=ALU.mult),
                    r=[("ps", bank)] + gk, w=kKTA)
                for d in range(2):
                    for p in range(2):
                        P.op("pe", lambda e, KTA=KTA, d=d, p=p, c=c: e.matmul(
                            PS[:, d * 2 + p, 0:258], lhsT=KTA[:, d, 2 * p:2 * p + 2, :].rearrange("p h k -> p (h k)"),
                            rhs=VAUG[:, c, 2 * p:2 * p + 2, :].rearrange("p h v -> p (h v)"), start=(c == 0), stop=(c == 15)),
                            r=kKTA + kVA(c), w=[("ps", d * 2 + p)])
            for d in range(2):
                for p in range(2):
                    i = d * 2 + p
                    P.op("dve", lambda e, i=i: e.tensor_copy(out=EXS[0:64, i * 129:(i + 1) * 129], in_=PS[0:64, i, 0:129]), r=[("ps", i)], w=kEXS)
                    P.op("dve", lambda e, i=i: e.tensor_copy(out=EXS[64:128, i * 129:(i + 1) * 129], in_=PS[64:128, i, 129:258]), r=[("ps", i)], w=kEXS)
            P.op("dve", lambda e: e.tensor_copy(out=EXS[:, 516:520], in_=UB[:, :, 0]), r=["ub"], w=kEXS)
            P.op("dve", lambda e: e.tensor_copy(out=EXS[:, 520:524], in_=UB[:, :, 7]), r=["ub"], w=kEXS)

    mixer_front(True)
    dma("sp", pay_d, EXS, kEXS, ["pay"], "payo")
    for b in range(2):
        ffn(b, w1i, w1o, 0, 1)
    mixer_front(False)
    RQf_b = Wb[:, 8960:9472]; kRQ_b = kW(8960, 9472)
    RQ_b = RQf_b.rearrange("p (a b) -> p a b", a=2)
    RQs = [(RQ, kRQ), (RQ_b, kRQ_b)]
    P.op("dve", lambda e: e.memset(SBFf, 0.0), w=kSBF)
    P.op("dve", lambda e: e.memset(RQf, 0.0), w=kRQ)
    P.op("dve", lambda e: e.memset(RQf_b, 0.0), w=kRQ_b)
    dma("sp", ACCS, pay_d, ["pay"], kACCS, "payi")
    for (lo, hi, so) in [(0, 258, 0), (520, 524, 0), (258, 520, 1)]:
        P.op("dve", lambda e, lo=lo, hi=hi, so=so: e.tensor_scalar(out=ACCS[:, lo:hi], in0=ACCS[:, lo:hi], scalar1=SEL[:, so:so + 1],
                                                              scalar2=None, op0=ALU.mult), r=kACCS + ["sel"], w=kACCS)
    for d in range(2):
        for p in range(2):
            i = d * 2 + p
            SSt, kSS = SSTs[d][p]
            P.op("dve", lambda e, SSt=SSt, i=i: e.tensor_copy(out=SSt, in_=ACCS[:, i * 129:(i + 1) * 129]), r=kACCS, w=kSS)
            P.op("act", lambda e, i=i: e.activation(out=SBF[0:64, i, 0:129], in_=ACCS[0:64, i * 129:(i + 1) * 129], func=AF.Copy),
                 r=kACCS, w=ksbf(i))
            P.op("act", lambda e, i=i: e.activation(out=SBF[64:128, i, 129:258], in_=ACCS[64:128, i * 129:(i + 1) * 129], func=AF.Copy),
                 r=kACCS, w=ksbf(i))
    P.op("dve", lambda e: e.tensor_copy(out=HALO[:, :, 0], in_=ACCS[:, 520:524]), r=kACCS, w=["halo"])
    P.op("dve", lambda e: e.tensor_copy(out=HALO[:, :, 1], in_=ACCS[:, 516:520]), r=kACCS, w=["halo"])

    HS = R1[:].rearrange("p (c f) -> p c f", c=16)
    arrived = [0] * 16

    def kHS(c):
        return kR(c * 512, (c + 1) * 512)

    def fin_a(c):
        hsf = HS[:, c, :]
        P.op("dve", lambda e: e.tensor_tensor(out=SQT, in0=hsf, in1=hsf, op=ALU.mult), r=kHS(c), w=kSQT)
        P.op("dve", lambda e: e.tensor_reduce(out=SSH, in_=SQT.rearrange("p (h v) -> p h v", h=4), axis=AX.X, op=ALU.add), r=kSQT, w=["ssh"])
        P.op("act", lambda e: e.activation(out=SSH, in_=SSH, func=AF.Ln, scale=1.0 / 128, bias=EPS_AP), r=["ssh", "cst"], w=["ssh"])
        P.op("act", lambda e: e.activation(out=SSH, in_=SSH, func=AF.Exp, scale=-0.5), r=["ssh"], w=["ssh"])
        P.op("dve", lambda e: e.tensor_tensor(out=SQT.rearrange("p (h v) -> p h v", h=4), in0=hsf.rearrange("p (h v) -> p h v", h=4),
                                             in1=bc(SSH, [[1, 4], [0, 128]]), op=ALU.mult), r=kHS(c) + ["ssh"], w=kSQT)
        P.op("dve", lambda e: e.tensor_tensor(out=YTOK, in0=SQT, in1=SGO[:, c, :], op=ALU.mult), r=kSQT + kSGO(c), w=kYTOK)

    def fin_b(c):
        cs = slice(c * 128, (c + 1) * 128)
        yb = PS[:, 1, 0:256].bitcast(BF16)
        for ft in range(4):
            P.op("pe", lambda e, ft=ft: e.transpose(out=yb[:, ft * 128:(ft + 1) * 128], in_=YTOK[:, ft * 128:(ft + 1) * 128], identity=IDENTB[:]),
                 r=kYTOK + ["identb"], w=[("ps", 1)])
        P.op("act", lambda e: e.activation(out=QKT[:, :, cs], in_=yb.rearrange("p (f t) -> p f t", f=4), func=AF.Copy),
             r=[("ps", 1)], w=kQKall(c * 128, (c + 1) * 128))

    pending_fin = []

    def rq_fill(c, par):
        cs = slice(c * 128, (c + 1) * 128)
        RQp, kRQp = RQs[par]
        for p in range(2):
            P.op("act", lambda e, p=p: e.activation(out=RQp[0:64, p, 0:128], in_=QKT[0:64, p, cs], func=AF.Copy),
                 r=kQK(p, c * 128, (c + 1) * 128), w=kRQp)
            P.op("act", lambda e, p=p: e.activation(out=RQp[64:128, p, 128:256], in_=QKT[64:128, p, cs], func=AF.Copy),
                 r=kQK(p, c * 128, (c + 1) * 128), w=kRQp)

    def stepA(c, d, par):
        cs = slice(c * 128, (c + 1) * 128)
        c0_, c1_ = c * 128, (c + 1) * 128
        j0 = d * 4
        PT, kPT = PTs[par]
        KT2, kKT2 = KT2s[par]
        msu = MASKS[:, 2 + d, :]
        P.op("dve", lambda e: e.tensor_tensor(out=L1, in0=bc(msu, [[0, 4], [1, 128]]), in1=bc(LF[:, c, j0:j0 + 4], [[1, 4], [0, 128]]), op=ALU.mult),
             r=gk + ["masks"], w=kL1)
        P.op("dve", lambda e: e.tensor_tensor(out=L2, in0=bc(MASKS[:, 4, :], [[0, 4], [1, 128]]), in1=bc(LIB[:, c, j0:j0 + 4], [[1, 4], [0, 128]]), op=ALU.mult),
             r=gk + ["masks"], w=kL2)
        P.op("dve", lambda e: e.tensor_tensor(out=L1, in0=L1, in1=L2, op=ALU.add), r=kL1 + kL2, w=kL1)
        yield
        kb = ktok_transposes(c, 5)
        for p in range(2):
            P.op("pe", lambda e, p=p: e.matmul(PS[:, 0, p * 256:(p + 1) * 256], lhsT=QKT[:, 2 + p, cs], rhs=RQs[par][0][:, p, :], start=True, stop=True),
                 r=kQK(2 + p, c0_, c1_) + RQs[par][1], w=[("ps", 0)])
        for h in range(4):
            P.op("pe", lambda e, h=h: e.matmul(PS[:, 1, h * 128:(h + 1) * 128], lhsT=L1[:, h, :], rhs=MASKS[:, d, :], start=True, stop=True),
                 r=kL1 + ["masks"], w=[("ps", 1)])
        yield
        P.op("act", lambda e: e.activation(out=E_t, in_=PS[:, 1, :], func=AF.Exp, bias=NBIG_AP), r=[("ps", 1), "cst"], w=kE)
        yield
        P.op("dve", lambda e: e.tensor_tensor(out=PT, in0=E_t, in1=PS[:, 0, :], op=ALU.mult), r=kE + [("ps", 0)], w=kPT)
        P.op("dve", lambda e: e.tensor_tensor(out=KT2, in0=bc(kb, [[64, 4], [1, 64]]), in1=bc(EW[:, c, j0:j0 + 4], [[1, 4], [0, 64]]), op=ALU.mult),
             r=[("ps", 5)] + gk, w=kKT2)
        yield

    def stepB(c, d, par):
        cs = slice(c * 128, (c + 1) * 128)
        c0_, c1_ = c * 128, (c + 1) * 128
        j0 = d * 4
        PT, kPT = PTs[par]
        KT2, kKT2 = KT2s[par]
        for h in range(4):
            outp = PS[:, 4, h * 129:(h + 1) * 129] if h < 3 else PS[:, 7, 258:387]
            P.op("pe", lambda e, h=h, outp=outp: e.matmul(outp, lhsT=PT[:, h * 128:(h + 1) * 128], rhs=VAUG[:, c, h, :], start=True, stop=True),
                 r=kPT + kVA(c), w=[("ps", 4 if h < 3 else 7)])
        regs = [PS[:, 2, 0:258], PS[:, 3, 0:258]]
        for p in range(2):
            P.op("pe", lambda e, p=p: e.matmul(regs[p], lhsT=KT2[:, 2 * p:2 * p + 2, :].rearrange("p h k -> p (h k)"),
                                             rhs=VAUG[:, c, 2 * p:2 * p + 2, :].rearrange("p h v -> p (h v)"), start=True, stop=True),
                 r=kKT2 + kVA(c), w=[("ps", 2 + p)])
        for p in range(2):
            P.op("pe", lambda e, p=p: e.matmul(PS[:, 6 + p, 0:258], lhsT=QKT[:, p, cs], rhs=SBF[:, d * 2 + p, :], start=True, stop=True),
                 r=kQK(p, c0_, c1_) + ksbf(d * 2 + p), w=[("ps", 6 + p)])
        yield
        for h in range(4):
            inp = PS[:, 6 + h // 2, (h % 2) * 129:(h % 2 + 1) * 129]
            P.op("act", lambda e, h=h, inp=inp: e.activation(out=TI[:, h, :], in_=inp, func=AF.Copy, scale=EB[:, c, j0 + h:j0 + h + 1]),
                 r=[("ps", 6 + h // 2)] + gk, w=kTI)
        yield
        P.op("dve", lambda e: e.tensor_tensor(out=ND[:, 0:3, :], in0=TI[:, 0:3, :], in1=PS[:, 4, 0:387].rearrange("p (h v) -> p h v", h=3), op=ALU.add),
             r=kTI + [("ps", 4)], w=kND)
        P.op("dve", lambda e: e.tensor_tensor(out=ND[:, 3, :], in0=TI[:, 3, :], in1=PS[:, 7, 258:387], op=ALU.add), r=kTI + [("ps", 7)], w=kND)
        P.op("act", lambda e: e.activation(out=DEN, in_=ND[:, :, 128], func=AF.Abs), r=kND, w=["den"])
        P.op("dve", lambda e: e.tensor_scalar(out=DEN, in0=DEN, scalar1=1.0, scalar2=None, op0=ALU.max), r=["den"], w=["den"])
        P.op("dve", lambda e: e.reciprocal(out=RD, in_=DEN), r=["den"], w=["rd"])
        rdb = bc(RD, [[1, 4], [0, 128]])
        hsv = HS[:, c, :].rearrange("p (h v) -> p h v", h=4)
        if arrived[c] == 0:
            P.op("dve", lambda e: e.tensor_tensor(out=hsv, in0=ND[:, :, 0:128], in1=rdb, op=ALU.mult), r=kND + ["rd"], w=kHS(c))
        else:
            P.op("dve", lambda e: e.tensor_tensor(out=HHt, in0=ND[:, :, 0:128], in1=rdb, op=ALU.mult), r=kND + ["rd"], w=kHH)
            P.op("dve", lambda e: e.tensor_tensor(out=hsv, in0=hsv, in1=HHt, op=ALU.add), r=kHH + kHS(c), w=kHS(c))
        arrived[c] += 1
        yield
        for p in range(2):
            reg = regs[p]
            bk = 2 + p
            i = d * 2 + p
            SSt, kSS = SSTs[d][p]
            for half in range(2):
                r0 = half * 64
                j = j0 + 2 * p + half
                P.op("dve", lambda e, SSt=SSt, reg=reg, r0=r0, j=j, half=half: e.scalar_tensor_tensor(
                    out=SSt[r0:r0 + 64, :], in0=SSt[r0:r0 + 64, :], scalar=EGt[r0:r0 + 64, c, j:j + 1],
                    in1=reg[r0:r0 + 64, half * 129:(half + 1) * 129], op0=ALU.mult, op1=ALU.add),
                    r=[("ps", bk)] + kSS + gk, w=kSS)
            P.op("act", lambda e, SSt=SSt, i=i: e.activation(out=SBF[0:64, i, 0:129], in_=SSt[0:64, :], func=AF.Copy), r=kSS, w=ksbf(i))
            P.op("act", lambda e, SSt=SSt, i=i: e.activation(out=SBF[64:128, i, 129:258], in_=SSt[64:128, :], func=AF.Copy), r=kSS, w=ksbf(i))
        if arrived[c] == 2:
            fin_a(c)
            pending_fin.append(c)
        yield

    seq = []
    for i in range(16):
        seq += [(i, 0), (15 - i, 1)]
    rq_fill(seq[0][0], 0)
    rq_fill(seq[1][0], 1)
    for _ in stepA(*seq[0], 0):
        pass
    for n in range(32):
        gB = stepB(*seq[n], n % 2)
        gA = stepA(*seq[n + 1], (n + 1) % 2) if n + 1 < 32 else None
        next(gB)
        if gA: next(gA)
        next(gB)
        if gA: next(gA)
        next(gB)
        if gA: next(gA)
        if n + 2 < 32:
            rq_fill(seq[n + 2][0], n % 2)
        if pending_fin:
            fin_b(pending_fin.pop(0))
        next(gB)
        if gA: next(gA)
    while pending_fin:
        fin_b(pending_fin.pop(0))

    WCs = [wtile(i * 4096, 4096) for i in range(2)]
    WMO, kWMO = wtile(8192, 8192)
    wmo_v = wmo.rearrange("(k p) f -> p k f", p=128)
    set_slots(4)
    c1seq = [(hb, g) for hb in range(4) for g in range(3)]

    def enq_c(idx):
        WCw, kWCw = WCs[idx % 2]
        enqueue(WCw, wmi_v[:, :, c1seq[idx][1] * 512:(c1seq[idx][1] + 1) * 512], kWCw)
        return WCw, kWCw

    c1info = {0: enq_c(0)}
    enqueue(WMO, wmo_v, kWMO)
    MO = R1[:, 4096:8192].rearrange("p (m t) -> p m t", m=8)

    def kMO(m):
        return kR(4096 + m * 512, 4096 + (m + 1) * 512)
    ubidx = {0: (None, 1), 1: (2, 4), 2: (3, 5), 3: (6, None)}
    wcc = 0
    pcc = 0
    XNc = [XN[:, :, 0:512], XN[:, :, 512:1024]]

    def prenorm_c1(hb):
        par = hb % 2
        prenorm(hb * 512, 512, 2, XNc[par], lambda k: kXN(k, 1024) + [("xnc", par, k)])

    prenorm_c1(0)
    for hb in range(4):
        c0 = hb * 512
        par = hb % 2
        li_, ri_ = ubidx[hb]
        lsrc = HALO[:, :, 0] if li_ is None else UB[:, :, li_]
        rsrc = HALO[:, :, 1] if ri_ is None else UB[:, :, ri_]
        allU = [k_ for i in range(4) for k_ in kU(i)]
        P.op("dve", lambda e, lsrc=lsrc: e.tensor_copy(out=U[:, :, 0], in_=lsrc), r=["halo", "ub"], w=allU)
        P.op("dve", lambda e, rsrc=rsrc: e.tensor_copy(out=U[:, :, 513], in_=rsrc), r=["halo", "ub"], w=allU)

        def conv_ct(ct):
            P.op("dve", lambda e: e.tensor_scalar(out=CTMP, in0=U[:, ct, 0:512], scalar1=CONVP[:, ct, 0:1], scalar2=None, op0=ALU.mult),
                 r=kU(ct) + ["convp"], w=kCTMP)
            P.op("dve", lambda e: e.scalar_tensor_tensor(out=CTMP, in0=U[:, ct, 1:513], scalar=CONVP[:, ct, 1:2], in1=CTMP, op0=ALU.mult, op1=ALU.add),
                 r=kU(ct) + ["convp"] + kCTMP, w=kCTMP)
            P.op("dve", lambda e: e.scalar_tensor_tensor(out=CTMP, in0=U[:, ct, 2:514], scalar=CONVP[:, ct, 2:3], in1=CTMP, op0=ALU.mult, op1=ALU.add),
                 r=kU(ct) + ["convp"] + kCTMP, w=kCTMP)
            P.op("dve", lambda e: e.scalar_tensor_tensor(out=YC[:, ct, :], in0=CTMP, scalar=CONVP[:, ct, 3:4], in1=BG[:, ct, :], op0=ALU.add, op1=ALU.mult),
                 r=kCTMP + kBG(ct) + ["convp"], w=kYC(ct))

        for g in range(3):
            idx = hb * 3 + g
            cast_all()
            if idx + 1 < 12:
                c1info[idx + 1] = enq_c(idx + 1)
            WCw, kWCw = c1info[idx]
            for mt in range(4):
                pb = pcc % 2
                pcc += 1
                for k in range(8):
                    P.op("pe", lambda e, k=k, WCw=WCw, mt=mt, pb=pb, par=par: e.matmul(PS[:, pb, :], lhsT=WCw[:, k, mt * 128:(mt + 1) * 128], rhs=XNc[par][:, k, :],
                                                                           start=(k == 0), stop=(k == 7)),
                         r=kWCw + [("xnc", par, k)], w=[("ps", pb)])
                if g == 0:
                    P.op("act", lambda e, mt=mt, pb=pb: e.activation(out=BG[:, mt, :], in_=PS[:, pb, :], func=AF.Copy), r=[("ps", pb)], w=kBG(mt))
                elif g == 1:
                    P.op("act", lambda e, mt=mt, pb=pb: e.activation(out=U[:, mt, 1:513], in_=PS[:, pb, :], func=AF.Copy), r=[("ps", pb)], w=kU(mt))
                else:
                    P.op("dve", lambda e, mt=mt, pb=pb: e.tensor_tensor(out=U[:, mt, 1:513], in0=U[:, mt, 1:513], in1=PS[:, pb, :], op=ALU.mult),
                         r=[("ps", pb)] + kU(mt), w=kU(mt))
                    conv_ct(mt)
                cast_some(2)
        if hb + 1 < 4:
            prenorm_c1(hb + 1)
        for m in range(8):
            pb = 2 + m % 2
            for k in range(8):
                rhs = YC[:, k, :] if k < 4 else QKT[:, k - 4, c0:c0 + 512]
                rk = kYC(k) if k < 4 else kQK(k - 4, c0, c0 + 512)
                P.op("pe", lambda e, k=k, m=m, pb=pb, rhs=rhs: e.matmul(PS[:, pb, :], lhsT=WMO[:, k, m * 128:(m + 1) * 128], rhs=rhs, start=(k == 0), stop=(k == 7)),
                     r=kWMO + rk, w=[("ps", pb)])
            sq = SQ[m % 2][:, 0:512]
            P.op("act", lambda e, pb=pb, sq=sq: e.activation(out=sq, in_=PS[:, pb, :], func=AF.Square), r=[("ps", pb)], w=[("sq", m % 2)])
            P.op("pe", lambda e, sq=sq, m=m: e.matmul(PS[:, 6, :], lhsT=ONESB[:], rhs=sq, start=(m == 0), stop=(m == 7)),
                 r=[("sq", m % 2), "onesb"], w=[("ps", 6)])
            P.op("dve", lambda e, pb=pb, m=m: e.tensor_scalar(out=MO[:, m, :], in0=PS[:, pb, :], scalar1=GH[:, 3, m:m + 1], scalar2=None, op0=ALU.mult),
                 r=[("ps", pb), "gh", ("sq", m % 2)], w=kMO(m))
        rstd(512)
        for m in range(8):
            P.op("dve", lambda e, m=m: e.tensor_tensor(out=MO[:, m, :], in0=MO[:, m, :], in1=STD[:, 0:512], op=ALU.mult),
                 r=kMO(m) + ["std"], w=kMO(m))
            P.op("dve", lambda e, m=m, c0=c0: e.tensor_tensor(out=X[:, m, c0:c0 + 512], in0=X[:, m, c0:c0 + 512], in1=MO[:, m, :], op=ALU.add),
                 r=kMO(m) + [xk(hb, m)], w=[xk(hb, m)])
    if stage == 2:
        for b in range(2):
            store_out(b)
        return finish()

    for b in range(2):
        ffn(b, w2i, w2o, 4, 5)
        store_out(b)
    return finish()


_CACHE = {}


def _prep(inputs, stage):
    f = lambda a: np.ascontiguousarray(np.asarray(a, dtype=np.float32))
    x = f(inputs["x"])
    gains = np.stack([f(inputs[n])[0] for n in ["norm_ffn1_pre", "norm_ffn1_post", "norm_mix_pre", "norm_mix_post",
                                                 "norm_ffn2_pre", "norm_ffn2_post"]], 0)
    gains = np.ascontiguousarray(gains.reshape(6, 8, 128).transpose(2, 0, 1))
    cw = f(inputs["conv_w"])[0]
    cb = f(inputs["conv_b"])[0]
    convp = np.concatenate([cw, cb[None, :]], 0)
    convp = np.ascontiguousarray(convp.reshape(4, 4, 128).transpose(2, 1, 0))
    row = np.concatenate([f(inputs["mlstm_norm"])[0], f(inputs["gate_i_bias"])[0], f(inputs["gate_f_bias"])[0]])
    rowp = np.ascontiguousarray(np.broadcast_to(row[None, :], (128, 528)))
    r = np.arange(128)
    masks = np.stack([(r[:, None] <= r[None, :]), (r[:, None] >= r[None, :]), (r[:, None] > r[None, :]),
                      (r[:, None] < r[None, :]), (r[:, None] == r[None, :])], 1).astype(np.float32)
    identb = np.eye(128, dtype=np.float32).astype(ml_dtypes.bfloat16)
    common = {
        "w1i": (f(inputs["w_ffn1_in"])[0][:, :512].copy() if os.environ.get("KTINY") else f(inputs["w_ffn1_in"])[0]),
        "w1o": (f(inputs["w_ffn1_out"])[0][:, :128].copy() if os.environ.get("KTINY") else f(inputs["w_ffn1_out"])[0]),
        "wmi": f(inputs["w_mix_in"])[0], "wmo": f(inputs["w_mix_out"])[0],
        "w2i": f(inputs["w_ffn2_in"])[0], "w2o": f(inputs["w_ffn2_out"])[0],
        "gains": gains, "convp": convp, "rowp": rowp, "masks": np.ascontiguousarray(masks), "identb": identb,
    }
    in_maps = []
    for c in range(NCORES):
        bidx, half = c // 2, c % 2
        def lay(hh):
            xs = x[bidx, hh * NT:(hh + 1) * NT, :]
            return np.ascontiguousarray(xs.T.reshape(8, 128, NT).transpose(1, 0, 2))
        sel = np.zeros((128, 16), np.float32)
        sel[:, 0] = 1.0 if half == 1 else 0.0
        sel[:, 1] = 1.0 if half == 0 else 0.0
        m = dict(common)
        m["xT"] = lay(half)
        m["xP"] = lay(1 - half)
        m["sel"] = sel
        in_maps.append(m)
    return in_maps


def run(inputs, stage=STAGE_FULL):
    if stage not in _CACHE:
        _CACHE[stage] = build(stage)
    nc = _CACHE[stage]
    in_maps = _prep(inputs, stage)
    res = run_bass_kernel_spmd(nc, in_maps, core_ids=list(range(NCORES)))
    out = np.empty((4, 4096, D), np.float32)
    for c in range(NCORES):
        yt = np.asarray(res.results[c]["yT"])
        out[c // 2, (c % 2) * NT:(c % 2 + 1) * NT, :] = yt.transpose(1, 0, 2).reshape(D, NT).T
    return out


def kernel(**inputs):
    return run(inputs, STAGE_FULL)
```
